# Optimizing a Trainium2 kernel written in Bass

```python
import jax, jax.numpy as jnp
from jax import lax
import numpy as np

D_MODEL = 4096
BATCH = 16
SEQ = 256
DEPTH = 2
DEC_BATCH = 2
DEC_SEQ = 4096
PAST_LEN = 512

GRID_W = 64
HEAD_DIM = 128
N_Q_A = 12
N_KV_A = 4
GQA_GROUP = N_Q_A // N_KV_A
WINDOW = 128
Q_BLOCK = 128
N_HEADS_B = 12
NA_ROWS = 8
NA_COLS = 16
CONV_CH = 1024
CONV_WIDTH = 3
D_FF = 11008
N_MOD = 9
ROPE_BASE = 10000.0
AXIS_DIM = HEAD_DIM // 2
ROPE_PAIRS = AXIS_DIM // 2
EPS = 1e-6
NEG_INF = -1e30
ATTN_SCALE = HEAD_DIM ** -0.5

WIDTH_A_Q = N_Q_A * HEAD_DIM
WIDTH_A_KV = N_KV_A * HEAD_DIM
WIDTH_B = N_HEADS_B * HEAD_DIM
IN_SIZES = (WIDTH_A_Q, WIDTH_A_KV, WIDTH_A_KV, WIDTH_B, WIDTH_B, WIDTH_B, CONV_CH, CONV_CH, CONV_CH)
IN_WIDTH = sum(IN_SIZES)
MIX_OUT = WIDTH_A_Q + WIDTH_B + CONV_CH

kernel_name = 'hybrid_dit_prefix_context_step'


def _normal(key, shape, scale):
    return jax.random.normal(key, shape, jnp.float32) * scale


def _rms(x, g):
    xf = x.astype(jnp.float32)
    y = xf * lax.rsqrt(jnp.mean(xf * xf, axis=-1, keepdims=True) + EPS)
    return (y * g.astype(jnp.float32)).astype(x.dtype)


def _modulate(x, shift, scale):
    return x * (1 + scale) + shift


def _modulation(cvec, mod_w, mod_b):
    m = jax.nn.silu(cvec) @ mod_w + mod_b
    m = m.reshape(m.shape[:-1] + (N_MOD, D_MODEL))
    return [jnp.expand_dims(m[..., i, :], -2) for i in range(N_MOD)]


def _ffn_sublayer(h, shift, scale, gate, g, w13, w2):
    x = _modulate(_rms(h, g), shift, scale)
    a, b = jnp.split(x @ w13, 2, axis=-1)
    return h + 0.5 * gate * ((jax.nn.silu(a) * b) @ w2)


def _softmax_sink(s, sink):
    m = jnp.max(s, axis=-1, keepdims=True)
    if sink is not None:
        m = jnp.maximum(m, sink)
    e = jnp.exp(s - m)
    den = jnp.sum(e, axis=-1, keepdims=True)
    if sink is not None:
        den = den + jnp.exp(sink - m)
    return e / den


def _split_proj(p):
    points = np.cumsum(IN_SIZES)[:-1].tolist()
    return jnp.split(p, points, axis=-1)


def _axial_rope(n):
    t = jnp.arange(n)
    pos = jnp.stack([t // GRID_W, t % GRID_W], axis=-1).astype(jnp.float32)
    inv = ROPE_BASE ** (-jnp.arange(ROPE_PAIRS, dtype=jnp.float32) * 2.0 / AXIS_DIM)
    ang = pos[:, :, None] * inv
    return jnp.cos(ang), jnp.sin(ang)


def _apply_rope(x, cos, sin):
    sh = x.shape
    xf = x.astype(jnp.float32).reshape(sh[:-1] + (2, 2, ROPE_PAIRS))
    lo, hi = xf[..., 0, :], xf[..., 1, :]
    c = cos[None, :, None]
    s = sin[None, :, None]
    out = jnp.stack([lo * c - hi * s, hi * c + lo * s], axis=-2)
    return out.reshape(sh).astype(x.dtype)


def _short_conv(x, gate_b, gate_c, w, b):
    u = gate_c * x
    y = lax.conv_general_dilated(u, w[:, None, :], window_strides=(1,),
                                 padding=[(CONV_WIDTH // 2, CONV_WIDTH // 2)],
                                 dimension_numbers=('NWC', 'WIO', 'NWC'),
                                 feature_group_count=CONV_CH) + b
    return gate_b * y


def _dense_attention(q, k, v, sink):
    B, S, KV, G, d = q.shape
    nb = S // Q_BLOCK
    qb = jnp.moveaxis(q.reshape(B, nb, Q_BLOCK, KV, G, d), 1, 0)
    sink_f = None if sink is None else sink.astype(jnp.float32)[None, :, :, None, None]

    def blk(qi):
        s = jnp.einsum('bqkgd,bskd->bkgqs', qi, k, preferred_element_type=jnp.float32) * ATTN_SCALE
        p = _softmax_sink(s, sink_f).astype(v.dtype)
        return jnp.einsum('bkgqs,bskd->bqkgd', p, v)

    o = lax.map(blk, qb)
    return jnp.moveaxis(o, 0, 1).reshape(B, S, KV * G * d)


def _window_attention(q, k, v, ck, cv, sink):
    B, N = q.shape[0], q.shape[1]
    nb = N // Q_BLOCK
    span = Q_BLOCK + 2 * WINDOW
    pad = ((0, 0), (WINDOW, WINDOW), (0, 0), (0, 0))
    kp = jnp.pad(k, pad)
    vp = jnp.pad(v, pad)
    sink_f = sink.astype(jnp.float32)[None, :, :, None, None]
    rel = jnp.arange(span)[None, :] - jnp.arange(Q_BLOCK)[:, None]
    band = (rel >= 0) & (rel <= 2 * WINDOW)

    def blk(i):
        start = i * Q_BLOCK
        qi = lax.dynamic_slice_in_dim(q, start, Q_BLOCK, axis=1)
        ki = lax.dynamic_slice_in_dim(kp, start, span, axis=1)
        vi = lax.dynamic_slice_in_dim(vp, start, span, axis=1)
        kpos = start - WINDOW + jnp.arange(span)
        valid = band & ((kpos >= 0) & (kpos < N))[None, :]
        s_loc = jnp.einsum('bqkgd,bskd->bkgqs', qi, ki, preferred_element_type=jnp.float32) * ATTN_SCALE
        s_loc = jnp.where(valid, s_loc, NEG_INF)
        s_ctx = jnp.einsum('bqkgd,bskd->bkgqs', qi, ck, preferred_element_type=jnp.float32) * ATTN_SCALE
        p = _softmax_sink(jnp.concatenate([s_loc, s_ctx], axis=-1), sink_f).astype(v.dtype)
        return (jnp.einsum('bkgqs,bskd->bqkgd', p[..., :span], vi)
                + jnp.einsum('bkgqs,bskd->bqkgd', p[..., span:], cv))

    o = lax.map(blk, jnp.arange(nb))
    return jnp.moveaxis(o, 0, 1).reshape(B, N, -1)


def _neighbourhood_attention(q, k, v, ck, cv, rpb):
    B, N, H, d = q.shape
    rows = N // GRID_W
    kr = min(NA_ROWS, rows)
    qg = q.reshape(B, rows, GRID_W, H, d)
    kg = k.reshape(B, rows, GRID_W, H, d)
    vg = v.reshape(B, rows, GRID_W, H, d)
    qc = jnp.arange(GRID_W)
    cs = jnp.clip(qc - NA_COLS // 2, 0, GRID_W - NA_COLS)
    col_ok = (qc[None, :] >= cs[:, None]) & (qc[None, :] < cs[:, None] + NA_COLS)
    dc_idx = jnp.clip(qc[None, :] - qc[:, None], -(NA_COLS - 1), NA_COLS - 1) + NA_COLS - 1
    mask = jnp.broadcast_to(col_ok[:, None, :], (GRID_W, kr, GRID_W)).reshape(GRID_W, kr * GRID_W)
    rpb_f = rpb.astype(jnp.float32)

    def row(r):
        rs = jnp.clip(r - kr // 2, 0, rows - kr)
        qr = lax.dynamic_index_in_dim(qg, r, axis=1, keepdims=False)
        kr_blk = lax.dynamic_slice_in_dim(kg, rs, kr, axis=1).reshape(B, kr * GRID_W, H, d)
        vr_blk = lax.dynamic_slice_in_dim(vg, rs, kr, axis=1).reshape(B, kr * GRID_W, H, d)
        dr_idx = rs + jnp.arange(kr) - r + NA_ROWS - 1
        bias = rpb_f[:, dr_idx[:, None, None], dc_idx[None, :, :]]
        bias = jnp.transpose(bias, (0, 2, 1, 3)).reshape(H, GRID_W, kr * GRID_W)
        s_loc = jnp.einsum('bqhd,bkhd->bhqk', qr, kr_blk, preferred_element_type=jnp.float32) * ATTN_SCALE + bias
        s_loc = jnp.where(mask, s_loc, NEG_INF)
        s_ctx = jnp.einsum('bqhd,bkhd->bhqk', qr, ck, preferred_element_type=jnp.float32) * ATTN_SCALE
        p = _softmax_sink(jnp.concatenate([s_loc, s_ctx], axis=-1), None).astype(v.dtype)
        nl = kr * GRID_W
        return (jnp.einsum('bhqk,bkhd->bqhd', p[..., :nl], vr_blk)
                + jnp.einsum('bhqk,bkhd->bqhd', p[..., nl:], cv))

    o = lax.map(row, jnp.arange(rows))
    return jnp.moveaxis(o, 0, 1).reshape(B, N, H * d)


def _mix_context(u, w_in, w_out, aqn, akn, sink, bqn, bkn, conv_w, conv_b):
    B, S, _ = u.shape
    qa, ka, va, qb, kb, vb, xc, gb, gc = _split_proj(u @ w_in)
    qa = _rms(qa.reshape(B, S, N_Q_A, HEAD_DIM), aqn).reshape(B, S, N_KV_A, GQA_GROUP, HEAD_DIM)
    ka = _rms(ka.reshape(B, S, N_KV_A, HEAD_DIM), akn)
    va = va.reshape(B, S, N_KV_A, HEAD_DIM)
    qb = _rms(qb.reshape(B, S, N_HEADS_B, HEAD_DIM), bqn)[:, :, :, None, :]
    kb = _rms(kb.reshape(B, S, N_HEADS_B, HEAD_DIM), bkn)
    vb = vb.reshape(B, S, N_HEADS_B, HEAD_DIM)
    oa = _dense_attention(qa, ka, va, sink.reshape(N_KV_A, GQA_GROUP))
    ob = _dense_attention(qb, kb, vb, None)
    oc = _short_conv(xc, gb, gc, conv_w, conv_b)
    out = jnp.concatenate([oa, ob, oc], axis=-1) @ w_out
    return out, ka, va, kb, vb


def _mix_latent(u, ck_a, cv_a, ck_b, cv_b, cos, sin, w_in, w_out, aqn, akn, sink, bqn, bkn, rpb, conv_w, conv_b):
    B, N, _ = u.shape
    qa, ka, va, qb, kb, vb, xc, gb, gc = _split_proj(u @ w_in)
    qa = _apply_rope(_rms(qa.reshape(B, N, N_Q_A, HEAD_DIM), aqn), cos, sin)
    qa = qa.reshape(B, N, N_KV_A, GQA_GROUP, HEAD_DIM)
    ka = _apply_rope(_rms(ka.reshape(B, N, N_KV_A, HEAD_DIM), akn), cos, sin)
    va = va.reshape(B, N, N_KV_A, HEAD_DIM)
    qb = _rms(qb.reshape(B, N, N_HEADS_B, HEAD_DIM), bqn)
    kb = _rms(kb.reshape(B, N, N_HEADS_B, HEAD_DIM), bkn)
    vb = vb.reshape(B, N, N_HEADS_B, HEAD_DIM)
    oa = _window_attention(qa, ka, va, ck_a, cv_a, sink.reshape(N_KV_A, GQA_GROUP))
    ob = _neighbourhood_attention(qb, kb, vb, ck_b, cv_b, rpb)
    oc = _short_conv(xc, gb, gc, conv_w, conv_b)
    return jnp.concatenate([oa, ob, oc], axis=-1) @ w_out


def setup_inputs(seed: int = 0) -> dict:
    key = jax.random.key(seed)
    ks = jax.random.split(key, 27)
    D, L = D_MODEL, DEPTH
    return {
        'x_prompt': _normal(ks[0], (BATCH, SEQ, D), 1.0),
        'x_sample': _normal(ks[1], (DEC_BATCH, DEC_SEQ, D), 1.0),
        'cache_a_k': _normal(ks[2], (DEC_BATCH, L, PAST_LEN, N_KV_A, HEAD_DIM), 1.0),
        'cache_a_v': _normal(ks[3], (DEC_BATCH, L, PAST_LEN, N_KV_A, HEAD_DIM), 1.0),
        'cache_b_k': _normal(ks[4], (DEC_BATCH, L, PAST_LEN, N_HEADS_B, HEAD_DIM), 1.0),
        'cache_b_v': _normal(ks[5], (DEC_BATCH, L, PAST_LEN, N_HEADS_B, HEAD_DIM), 1.0),
        'c': _normal(ks[6], (DEC_BATCH, D), 1.0),
        'c_ctx': _normal(ks[7], (D,), 1.0),
        'mod_w': _normal(ks[8], (L, D, N_MOD * D), 0.5 * D ** -0.5),
        'mod_b': _normal(ks[9], (L, N_MOD * D), 0.01),
        'norm_ffn1': 1.0 + _normal(ks[10], (L, D), 0.01),
        'norm_mix': 1.0 + _normal(ks[11], (L, D), 0.01),
        'norm_ffn2': 1.0 + _normal(ks[12], (L, D), 0.01),
        'ffn1_w13': _normal(ks[13], (L, D, 2 * D_FF), D ** -0.5),
        'ffn1_w2': _normal(ks[14], (L, D_FF, D), D_FF ** -0.5),
        'ffn2_w13': _normal(ks[15], (L, D, 2 * D_FF), D ** -0.5),
        'ffn2_w2': _normal(ks[16], (L, D_FF, D), D_FF ** -0.5),
        'w_in': _normal(ks[17], (L, D, IN_WIDTH), D ** -0.5),
        'w_out': _normal(ks[18], (L, MIX_OUT, D), MIX_OUT ** -0.5),
        'a_q_norm': 1.0 + _normal(ks[19], (L, HEAD_DIM), 0.01),
        'a_k_norm': 1.0 + _normal(ks[20], (L, HEAD_DIM), 0.01),
        'a_sink': _normal(ks[21], (L, N_Q_A), 0.5),
        'b_q_norm': 1.0 + _normal(ks[22], (L, HEAD_DIM), 0.01),
        'b_k_norm': 1.0 + _normal(ks[23], (L, HEAD_DIM), 0.01),
        'b_rpb': _normal(ks[24], (L, N_HEADS_B, 2 * NA_ROWS - 1, 2 * NA_COLS - 1), 0.1),
        'conv_w': _normal(ks[25], (L, CONV_WIDTH, CONV_CH), CONV_WIDTH ** -0.5),
        'conv_b': _normal(ks[26], (L, CONV_CH), 0.01),
    }


def reference(x_prompt, x_sample, cache_a_k, cache_a_v, cache_b_k, cache_b_v, c, c_ctx,
              mod_w, mod_b, norm_ffn1, norm_mix, norm_ffn2, ffn1_w13, ffn1_w2, ffn2_w13, ffn2_w2,
              w_in, w_out, a_q_norm, a_k_norm, a_sink, b_q_norm, b_k_norm, b_rpb, conv_w, conv_b):
    cos, sin = _axial_rope(x_sample.shape[1])
    hp, hs = x_prompt, x_sample
    new_ak, new_av, new_bk, new_bv = [], [], [], []
    for l in range(DEPTH):
        mp = _modulation(c_ctx, mod_w[l], mod_b[l])
        ms = _modulation(c, mod_w[l], mod_b[l])
        hp = _ffn_sublayer(hp, mp[0], mp[1], mp[2], norm_ffn1[l], ffn1_w13[l], ffn1_w2[l])
        hs = _ffn_sublayer(hs, ms[0], ms[1], ms[2], norm_ffn1[l], ffn1_w13[l], ffn1_w2[l])
        up = _modulate(_rms(hp, norm_mix[l]), mp[3], mp[4])
        op, ka, va, kb, vb = _mix_context(up, w_in[l], w_out[l], a_q_norm[l], a_k_norm[l], a_sink[l],
                                          b_q_norm[l], b_k_norm[l], conv_w[l], conv_b[l])
        hp = hp + mp[5] * op
        new_ak.append(ka)
        new_av.append(va)
        new_bk.append(kb)
        new_bv.append(vb)
        us = _modulate(_rms(hs, norm_mix[l]), ms[3], ms[4])
        os_ = _mix_latent(us, cache_a_k[:, l], cache_a_v[:, l], cache_b_k[:, l], cache_b_v[:, l], cos, sin,
                          w_in[l], w_out[l], a_q_norm[l], a_k_norm[l], a_sink[l],
                          b_q_norm[l], b_k_norm[l], b_rpb[l], conv_w[l], conv_b[l])
        hs = hs + ms[5] * os_
        hp = _ffn_sublayer(hp, mp[6], mp[7], mp[8], norm_ffn2[l], ffn2_w13[l], ffn2_w2[l])
        hs = _ffn_sublayer(hs, ms[6], ms[7], ms[8], norm_ffn2[l], ffn2_w13[l], ffn2_w2[l])
    return (hp, hs, jnp.stack(new_ak, axis=1), jnp.stack(new_av, axis=1),
            jnp.stack(new_bk, axis=1), jnp.stack(new_bv, axis=1))
```

```python
import numpy as np
from contextlib import ExitStack
import ml_dtypes
import concourse.bass as bass
import concourse.mybir as mybir
from concourse.bass_utils import run_bass_kernel_spmd

F32 = mybir.dt.float32
BF16 = mybir.dt.bfloat16
AF = mybir.ActivationFunctionType
ALU = mybir.AluOpType

D = 4096
DFF = 11008
NL = 2
NCH = 32
GCH = 86
T = 512
NTOK = 2560
POFF = 2048
EPS = 1e-6
NQA, NKVA, NHB, HD = 12, 4, 12, 128
INW = 10240
NMOD = 9
SCALE = HD ** -0.5
COMPUTE = ("pe", "act", "dve")
DMAQ = ("sp", "pool")

R_CVEC, R_MODB, R_NORM, R_CONVW, R_CONVB, R_QKN, R_TOT = 0, 64, 640, 832, 880, 896, 1024


class _Op:
    __slots__ = ("eng", "fn", "deps", "sig", "cnt", "key", "val")


class Sched:
    def __init__(self):
        self.ops = {e: [] for e in COMPUTE + DMAQ}
        self.lastw = {}
        self.rd = {}
        self.lastdma = {}
        self.dmacnt = {}
        self.keyq = {}

    def add(self, eng, fn, R=(), W=(), key=None):
        op = _Op()
        op.eng, op.fn, op.sig, op.key, op.cnt, op.val = eng, fn, False, key, 0, 0
        deps = {}

        def dep(o):
            if o is None or (o.eng == "pe" and eng == "pe"):
                return
            deps[id(o)] = o

        for r in R:
            dep(self.lastw.get(r))
        for w in W:
            dep(self.lastw.get(w))
            for o in self.rd.get(w, ()):
                dep(o)
        if key is not None:
            dep(self.lastdma.get(key))
            self.lastdma[key] = op
            self.dmacnt[key] = self.dmacnt.get(key, 0) + 1
            op.val = 16 * self.dmacnt[key]
            self.keyq[key] = eng
        op.deps = list(deps.values())
        for o in op.deps:
            if o.key is None:
                o.sig = True
        for w in W:
            self.lastw[w] = op
            self.rd[w] = []
        for r in R:
            lst = self.rd.setdefault(r, [])
            if eng in COMPUTE:
                lst[:] = [o for o in lst if o.eng != eng]
            lst.append(op)
        self.ops[eng].append(op)

    def barrier(self):
        keys = set(self.lastw) | set(self.rd)
        self._bar = getattr(self, "_bar", 0) + 1
        return keys

    def emit(self, nc, block, es):
        prog = {e: es.enter_context(nc.semaphore("pg_" + e)) for e in COMPUTE}
        dsem = {k: es.enter_context(nc.semaphore("dq%d" % i)) for i, k in enumerate(self.dmacnt)}
        for e in COMPUTE:
            c = 0
            for op in self.ops[e]:
                if op.sig:
                    c += 1
                op.cnt = c

        def run(e, h):
            waited = {}
            for op in self.ops[e]:
                for o in op.deps:
                    if o.key is not None:
                        sem, v = dsem[o.key], o.val
                    else:
                        sem, v = prog[o.eng], o.cnt
                    if waited.get(id(sem), 0) < v:
                        h.wait_ge(sem, v)
                        waited[id(sem)] = v
                ins = op.fn(h)
                if op.key is not None:
                    ins.then_inc(dsem[op.key], 16)
                elif op.sig:
                    ins.then_inc(prog[e], 1)
            if e in DMAQ:
                for k, q in self.keyq.items():
                    if q == e and waited.get(id(dsem[k]), 0) < 16 * self.dmacnt[k]:
                        h.wait_ge(dsem[k], 16 * self.dmacnt[k])

        @block.tensor
        def _(h):
            run("pe", h)

        @block.scalar
        def _(h):
            run("act", h)

        @block.vector
        def _(h):
            run("dve", h)

        @block.sync
        def _(h):
            run("sp", h)

        @block.gpsimd
        def _(h):
            run("pool", h)


DECL = []
import os as _os
MSUB = int(_os.environ.get('MSUB', '127'))


def hres(c, segs):
    out = []
    for off, ln in segs:
        for j in range(off // 256, (off + ln - 1) // 256 + 1):
            out.append(("hT", c, j))
    return out


def sres(name, idx, segs):
    out = []
    for off, ln in segs:
        for j in range(off // 256, (off + ln - 1) // 256 + 1):
            out.append((name, idx, j))
    return out


def build(stage=99):
    nc = bass.Bass("TRN2", target_bir_lowering=False)
    S = Sched()

    DECL.clear()

    def din(name, shape, dt=F32):
        DECL.append(name)
        return nc.dram_tensor(name, list(shape), dt, kind="ExternalInput").ap()

    def dout(name, shape):
        return nc.dram_tensor(name, list(shape), F32, kind="ExternalOutput").ap()

    xs_d = din("xs", [2048, D])
    xp_d = din("xp", [512, D])
    small_d = din("small", [R_TOT, 128])
    ident_d = din("ident", [128, 128])
    swap_d = din("swapm", [128, 128], BF16)
    ropeC_d = din("ropeC", [128, 2048])
    ropeS_d = din("ropeS", [128, 2048])
    tokv_d = din("tokvalid", [128, 2048])
    maskA_d = din("maskA", [16, 128, 384], BF16)
    maskB_d = din("maskB", [16, 128, 896], BF16)
    Tz_d = din("Tz", [NL, 15, 64, 12, 64])
    sinkb_d = din("sinkb", [NL, 128, 12])
    cak_d = din("cak", [NL, 512, 512])
    cav_d = din("cav", [NL, 512, 512])
    cbk_d = din("cbk", [NL, 512, 1536])
    cbv_d = din("cbv", [NL, 512, 1536])
    if stage == 99:
        modw_d = din("mod_w", [NL, D, NMOD * D])
        w13_d = [din("ffn1_w13", [NL, D, 2 * DFF]), din("ffn2_w13", [NL, D, 2 * DFF])]
        w2_d = [din("ffn1_w2", [NL, DFF, D]), din("ffn2_w2", [NL, DFF, D])]
    win_d = din("w_in", [NL, D, INW])
    wout_d = din("w_out", [NL, D, D])
    ys_d = dout("ys", [1024, D])
    yp_d = dout("yp", [512, D])
    nak_d = dout("nak", [2, NL, 256, 512])
    nav_d = dout("nav", [2, NL, 256, 512])
    nbk_d = dout("nbk", [2, NL, 256, 1536])
    nbv_d = dout("nbv", [2, NL, 256, 1536])
    hT = nc.dram_tensor("hT", [NCH, 128, NTOK], F32).ap()
    qT_s = nc.dram_tensor("qT_s", [24, 128, NTOK], BF16).ap()
    kT_s = nc.dram_tensor("kT_s", [16, 128, NTOK], BF16).ap()
    V_s = nc.dram_tensor("V_s", [16, 128, 20, 128], BF16).ap()
    uT_s = nc.dram_tensor("uT_s", [8, 128, NTOK], F32).ap()
    gbT_s = nc.dram_tensor("gbT_s", [8, 128, NTOK], F32).ap()
    ckT_s = nc.dram_tensor("ckT_s", [16, 128, 512], BF16).ap()
    wc_t = [nc.dram_tensor("wcache%d" % i, [120, 128, 8192], BF16).ap() for i in range(6)]

    es = ExitStack()
    with es:
        E = es.enter_context
        WS = [E(nc.sbuf_tensor("ws%d" % i, [128, 32, 256], BF16)) for i in range(3)]
        xT = E(nc.sbuf_tensor("xT", [128, NCH, T], BF16))
        AR = E(nc.sbuf_tensor("arena", [128, GCH * T], BF16))
        Fr = E(nc.sbuf_tensor("fring", [128, 14, T], F32))
        SQ = E(nc.sbuf_tensor("sq", [128, 2, T], BF16))
        smallT = E(nc.sbuf_tensor("smallT", [128, R_TOT], F32))
        ident = E(nc.sbuf_tensor("identS", [128, 128], F32))
        ones = E(nc.sbuf_tensor("ones", [128, 128], BF16))
        swapm = E(nc.sbuf_tensor("swapS", [128, 128], BF16))
        scT = E(nc.sbuf_tensor("scT", [128, 32, 2], BF16))
        MODT = E(nc.sbuf_tensor("modT", [128, 2, 288], F32))
        DER = E(nc.sbuf_tensor("der", [128, 2, 3, 2, 32], F32))
        sinkE = E(nc.sbuf_tensor("sinkE", [128, 12], F32))
        fsc = E(nc.sbuf_tensor("fsc", [128, 2], F32))
        ps = [E(nc.psum_tensor("ps%d" % i, [128, T], F32)) for i in range(8)]
        block = E(nc.Block())

        def gT(n):
            return AR[:, n * T:(n + 1) * T]

        def Fs(i):
            return Fr[:, i, :]

        def F2(i):
            return Fr[:, i:i + 2, :].rearrange("p a b -> p (a b)")

        EBv = AR[:, 0:10752].rearrange("p (h f) -> p h f", f=896)
        mAt = AR[:, 10752:12288].rearrange("p (q f) -> p q f", f=384)
        mBt = AR[:, 12288:15872].rearrange("p (q f) -> p q f", f=896)
        KTw = [AR[:, 15872 + i * 1280:15872 + (i + 1) * 1280] for i in range(2)]
        Vw = [AR[:, 18432 + i * 1280:18432 + (i + 1) * 1280].rearrange("p (k d) -> p k d", d=128) for i in range(2)]
        cK = [AR[:, 20992 + i * 512:20992 + (i + 1) * 512] for i in range(2)]
        cVb = [AR[:, 22016 + i * 512:22016 + (i + 1) * 512].rearrange("p (k d) -> p k d", d=128) for i in range(2)]
        qTh = [AR[:, 23040 + i * 512:23040 + (i + 1) * 512] for i in range(2)]
        PT = [AR[:, 24064 + i * 1408:24064 + (i + 1) * 1408] for i in range(2)]
        STG = [AR[:, 26880 + i * 512:26880 + (i + 1) * 512] for i in range(2)]
        VST = [AR[:, 27904 + i * 256:27904 + (i + 1) * 256] for i in range(2)]
        ATT_KEYS = (["EB", "mAt", "mBt"] + [(n, i) for n in ("KTw", "Vw", "cK", "cVb", "qTh", "PT", "STG", "VST")
                                            for i in range(2)])

        HL = [0, 1, 2, 3]
        HS = [4, 5, 6, 7]
        TT_ = [8, 9]
        RS = 10

        def mm(out, lhsT, rhs, start, stop, R, W):
            S.add("pe", lambda h: h.matmul(out, lhsT, rhs, start=start, stop=stop), R, W)

        def tr(out, in_, R, W):
            S.add("pe", lambda h: h.transpose(out, in_, ident[:]), list(R) + ["ident"], W)

        def act(out, in_, func, R, W, bias=0.0, scale=1.0):
            S.add("act", lambda h: h.activation(out, in_, func, bias=bias, scale=scale), R, W)

        def tt(out, a, b, op, R, W):
            S.add("dve", lambda h: h.tensor_tensor(out, a, b, op), R, W)

        def ts2(out, a, s1, s2, op0, op1, R, W):
            S.add("dve", lambda h: h.tensor_scalar(out, a, s1, s2, op0, op1), R, W)

        def stt(out, a, sc, b, op0, op1, R, W):
            S.add("dve", lambda h: h.scalar_tensor_tensor(out, a, sc, b, op0, op1), R, W)

        def cp(out, in_, R, W):
            S.add("dve", lambda h: h.tensor_copy(out, in_), R, W)

        def rcp(out, in_, R, W):
            S.add("dve", lambda h: h.reciprocal(out, in_), R, W)

        def mset(out, val, W):
            S.add("dve", lambda h: h.memset(out, val), (), W)

        def dma(q, out, in_, R, W, key):
            S.add(q, lambda h: h.dma_start(out=out, in_=in_), R, W, key=key)

        def arena_fence():
            mset(fsc[:, 0:1], 0.0, [("g", n) for n in range(GCH)] + ATT_KEYS + ["fsc"])

        wcnt = [0]

        wcache = {}

        def wload(view, col0, c0, c1, ckey=None):
            s = wcnt[0] % 3
            wcnt[0] += 1
            n = (c1 - c0) * 256
            flat = WS[s][:, :, :].rearrange("p c n -> p (c n)")
            if ckey is not None and ckey in wcache:
                sid = wcache[ckey]
                dma("pool", flat[:, 0:n], wc_t[sid // 120][sid % 120, :, 0:n], [("wc", sid)], [("W", s)], ("W", s))
            else:
                dma("pool", WS[s][:, 0:c1 - c0, :], view[:, c0:c1, col0:col0 + 256], (), [("W", s)], ("W", s))
                if ckey is not None:
                    sid = len(wcache)
                    wcache[ckey] = sid
                    dma("sp", wc_t[sid // 120][sid % 120, :, 0:n], flat[:, 0:n], [("W", s)], [("wc", sid)], ("wcst", s))
            return s

        def wview(w):
            return w.rearrange("(c p) n -> p c n", p=128)

        def seg_dma_in(slot, src3, idx, segs, rname):
            lo = 0
            for (off, ln) in segs:
                dma("sp", Fr[:, slot, lo:lo + ln], src3[idx, :, off:off + ln], sres(rname, idx, [(off, ln)]),
                    [("F", slot)], ("F", slot))
                lo += ln

        def seg_dma_out(slot, dst3, idx, segs, rname):
            lo = 0
            for (off, ln) in segs:
                dma("sp", dst3[idx, :, off:off + ln], Fr[:, slot, lo:lo + ln], [("F", slot)],
                    sres(rname, idx, [(off, ln)]), ("Fst", slot))
                lo += ln

        def load_h(c, slot, segs):
            seg_dma_in(slot, hT, c, segs, "hT")

        def store_h(c, slot, segs):
            seg_dma_out(slot, hT, c, segs, "hT")

        dma("sp", ident[:], ident_d[:, :], (), ["ident"], "ident")
        dma("sp", swapm[:], swap_d[:, :], (), ["swapm"], "swapm")
        mset(ones[:], 1.0, ["ones"])
        for b in range(R_TOT // 128):
            dma("sp", Fr[:, HL[b % 4], 0:128], small_d[b * 128:(b + 1) * 128, :], (), [("F", HL[b % 4])], ("F", HL[b % 4]))
            tr(ps[b // 4][:, (b % 4) * 128:(b % 4 + 1) * 128], Fr[:, HL[b % 4], 0:128], [("F", HL[b % 4])], [("ps", b // 4)])
            if b % 4 == 3:
                cp(smallT[:, (b // 4) * 512:(b // 4 + 1) * 512], ps[b // 4][:, :], [("ps", b // 4)], ["smallT"])
        for v in range(2):
            act(scT[:, :, v], smallT[:, R_CVEC + v * 32:R_CVEC + (v + 1) * 32], AF.Silu, ["smallT"], ["scT"])

        for w in range(5):
            src = xs_d if w < 4 else xp_d
            r0 = w * 512 if w < 4 else 0
            off = w * 512
            for cg in range(8):
                for tb in range(4):
                    dma("sp", Fs(HL[tb]), src[r0 + tb * 128:r0 + (tb + 1) * 128, cg * 512:(cg + 1) * 512], (),
                        [("F", HL[tb])], ("F", HL[tb]))
                for cc in range(4):
                    c = cg * 4 + cc
                    bk = 4 + c % 4
                    for tb in range(4):
                        tr(ps[bk][:, tb * 128:(tb + 1) * 128], Fr[:, HL[tb], cc * 128:(cc + 1) * 128],
                           [("F", HL[tb])], [("ps", bk)])
                    hs = HS[c % 4]
                    cp(Fs(hs), ps[bk][:, :], [("ps", bk)], [("F", hs)])
                    store_h(c, hs, [(off, 512)])

        def modulation(l):
            mv = wview(modw_d[l])
            for si in range(144):
                s = wload(mv, si * 256, 0, 32)
                for j in range(2):
                    blk = 2 * si + j
                    bank = blk // 256
                    col = 2 * (blk % 256)
                    for c in range(NCH):
                        mm(ps[bank][:, col:col + 2], WS[s][:, c, j * 128:(j + 1) * 128], scT[:, c, :],
                           c == 0, c == NCH - 1, [("W", s), "scT"], [("ps", bank)])
            for v in range(2):
                p0 = ps[0][:, :].rearrange("p (b v) -> p b v", v=2)
                p1 = ps[1][:, 0:64].rearrange("p (b v) -> p b v", v=2)
                mb = R_MODB + l * 288
                tt(MODT[:, v, 0:256], p0[:, :, v], smallT[:, mb:mb + 256], ALU.add, [("ps", 0), "smallT"], ["MODT"])
                tt(MODT[:, v, 256:288], p1[:, :, v], smallT[:, mb + 256:mb + 288], ALU.add, [("ps", 1), "smallT"], ["MODT"])
            for v in range(2):
                for s3 in range(3):
                    g0 = R_NORM + (l * 3 + s3) * 32
                    ts2(DER[:, v, s3, 0, :], MODT[:, v, (3 * s3 + 1) * 32:(3 * s3 + 2) * 32], 1.0, 1.0, ALU.add, ALU.mult,
                        ["MODT"], ["DER"])
                    tt(DER[:, v, s3, 0, :], DER[:, v, s3, 0, :], smallT[:, g0:g0 + 32], ALU.mult, ["DER", "smallT"], ["DER"])
                    ts2(DER[:, v, s3, 1, :], MODT[:, v, (3 * s3 + 2) * 32:(3 * s3 + 3) * 32], 0.5 if s3 != 1 else 1.0, 0.0,
                        ALU.mult, ALU.add, ["MODT"], ["DER"])

        def modA(v, s3, c):
            return DER[:, v, s3, 0, c:c + 1]

        def modG(v, s3, c):
            return DER[:, v, s3, 1, c:c + 1]

        def modB(v, s3, c):
            return MODT[:, v, 3 * s3 * 32 + c:3 * s3 * 32 + c + 1]

        def norm_tile(segs, v, s3):
            NB = 7
            for c in range(NCH):
                sl = HL[c % 4]
                load_h(c, sl, segs)
                act(SQ[:, c % 2, :], Fs(sl), AF.Square, [("F", sl)], [("SQ", c % 2)])
                mm(ps[NB][:, :], ones[:], SQ[:, c % 2, :], c == 0, c == NCH - 1, ["ones", ("SQ", c % 2)], [("ps", NB)])
            act(Fs(TT_[0]), ps[NB][:, :], AF.Sqrt, [("ps", NB)], [("F", TT_[0])], bias=EPS, scale=1.0 / D)
            rcp(Fs(RS), Fs(TT_[0]), [("F", TT_[0])], [("F", RS)])
            for c in range(NCH):
                sl = HL[c % 4]
                load_h(c, sl, segs)
                t = TT_[c % 2]
                tt(Fs(t), Fs(sl), Fs(RS), ALU.mult, [("F", sl), ("F", RS)], [("F", t)])
                act(xT[:, c, :], Fs(t), AF.Identity, [("F", t), "DER", "MODT"], [("xT", c)],
                    bias=modB(v, s3, c), scale=modA(v, s3, c))

        def ffn_tile(l, which, segs, v):
            s3 = 0 if which == 0 else 2
            norm_tile(segs, v, s3)
            wv13 = wview(w13_d[which][l])
            wv2 = wview(w2_d[which][l])
            for jj in range(43):
                sa = wload(wv13, jj * 256, 0, 32, ("w13a", which, l, jj))
                sb = wload(wv13, DFF + jj * 256, 0, 32, ("w13b", which, l, jj))
                for sub in range(2):
                    n = 2 * jj + sub
                    pa, pb = n % 2, 2 + n % 2
                    for c in range(NCH):
                        mm(ps[pa][:, :], WS[sa][:, c, sub * 128:(sub + 1) * 128], xT[:, c, :], c == 0, c == NCH - 1,
                           [("W", sa), ("xT", c)], [("ps", pa)])
                    for c in range(NCH):
                        mm(ps[pb][:, :], WS[sb][:, c, sub * 128:(sub + 1) * 128], xT[:, c, :], c == 0, c == NCH - 1,
                           [("W", sb), ("xT", c)], [("ps", pb)])
                    t = TT_[n % 2]
                    act(Fs(t), ps[pa][:, :], AF.Silu, [("ps", pa)], [("F", t)])
                    tt(gT(n), Fs(t), ps[pb][:, :], ALU.mult, [("F", t), ("ps", pb)], [("g", n)])
            for mi in range(16):
                for kp in range(3):
                    c0, c1 = 32 * kp, min(32 * kp + 32, GCH)
                    s = wload(wv2, mi * 256, c0, c1, ("w2", which, l, mi, kp))
                    for mb in range(2):
                        m = 2 * mi + mb
                        bk = 4 + m % 4
                        for ci in range(c1 - c0):
                            c = c0 + ci
                            mm(ps[bk][:, :], WS[s][:, ci, mb * 128:(mb + 1) * 128], gT(c), c == 0, c == GCH - 1,
                               [("W", s), ("g", c)], [("ps", bk)])
                for mb in range(2):
                    m = 2 * mi + mb
                    bk = 4 + m % 4
                    sl, hs = HL[m % 4], HS[m % 4]
                    load_h(m, sl, segs)
                    stt(Fs(hs), ps[bk][:, :], modG(v, s3, m), Fs(sl), ALU.mult, ALU.add,
                        [("ps", bk), "DER", ("F", sl)], [("F", hs)])
                    store_h(m, hs, segs)

        def mixer_prep(l):
            for kvh in range(16):
                src = cak_d[l][:, kvh * 128:(kvh + 1) * 128] if kvh < 4 else cbk_d[l][:, (kvh - 4) * 128:(kvh - 3) * 128]
                bk = kvh % 2
                for tb in range(4):
                    dma("sp", Fr[:, HL[tb], 0:128], src[tb * 128:(tb + 1) * 128, :], (), [("F", HL[tb])], ("F", HL[tb]))
                    tr(ps[bk][:, tb * 128:(tb + 1) * 128], Fr[:, HL[tb], 0:128], [("F", HL[tb])], [("ps", bk)])
                cp(STG[kvh % 2], ps[bk][:, :], [("ps", bk)], [("STG", kvh % 2)])
                dma("sp", ckT_s[kvh, :, :], STG[kvh % 2], [("STG", kvh % 2)], [("ckT", kvh)], ("STG", kvh % 2))
            EBraw = Fr[:, 4:7, :].rearrange("p a b -> p (a b)").rearrange("p (h q) -> p h q", q=128)
            for di, dl in enumerate(range(-3, 4)):
                for krl in range(2):
                    for qrl in range(2):
                        dr = 2 * dl + krl - qrl + 7
                        dma("sp", EBraw[krl * 64:(krl + 1) * 64, :, qrl * 64:(qrl + 1) * 64], Tz_d[l, dr, :, :, :], (),
                            [("F", 4), ("F", 5), ("F", 6)], ("F", 4))
                act(EBv[:, :, di * 128:(di + 1) * 128], EBraw, AF.Exp, [("F", 4), ("F", 5), ("F", 6)], ["EB"])
            dma("sp", Fr[:, 8, 0:12], sinkb_d[l, :, :], (), [("F", 8)], ("F", 8))
            act(sinkE[:, :], Fr[:, 8, 0:12], AF.Exp, [("F", 8)], ["sinkE"])

        ncnt = [0]

        def tok_blocks(segs):
            out = []
            for (off, ln) in segs:
                for t in range(ln // 128):
                    out.append(off + t * 128)
            return out

        def mixer_in_tile(l, segs, v, full, prompt):
            norm_tile(segs, v, 1)
            wv = wview(win_d[l])
            xr = [("xT", c) for c in range(NCH)]
            tbs = tok_blocks(segs)
            RC, RSn, XC0, TV = 0, 1, 2, RS
            if not prompt:
                lo = 0
                for (off, ln) in segs:
                    dma("sp", Fr[:, RC, lo:lo + ln], ropeC_d[:, off:off + ln], (), [("F", RC)], ("F", RC))
                    dma("sp", Fr[:, RSn, lo:lo + ln], ropeS_d[:, off:off + ln], (), [("F", RSn)], ("F", RSn))
                    dma("sp", Fr[:, TV, lo:lo + ln], tokv_d[:, off:off + ln], (), [("F", TV)], ("F", TV))
                    lo += ln

            def proj_fm(s, j, bk):
                for c in range(NCH):
                    mm(ps[bk][:, :], WS[s][:, c, j * 128:(j + 1) * 128], xT[:, c, :], c == 0, c == NCH - 1,
                       [("W", s), ("xT", c)], [("ps", bk)])

            def qk_slot(col0, kind, head0):
                s = wload(wv, col0, 0, 32, ("win", l, col0))
                for j in range(2):
                    n = ncnt[0]
                    ncnt[0] += 1
                    bk = n % 4
                    proj_fm(s, j, bk)
                    head = head0 + j
                    isq = kind in ("qa", "qb")
                    rope = (kind in ("qa", "ka")) and not prompt
                    gcol = R_QKN + l * 4 + {"qa": 0, "ka": 1, "qb": 2, "kb": 3}[kind]
                    gv = smallT[:, gcol:gcol + 1]
                    qf, t, nb, sq = 11 + n % 2, TT_[n % 2], 4 + n % 2, n % 2
                    act(Fs(qf), ps[bk][:, :], AF.Identity, [("ps", bk)], [("F", qf)])
                    act(SQ[:, sq, :], ps[bk][:, :], AF.Square, [("ps", bk)], [("SQ", sq)])
                    mm(ps[nb][:, :], ones[:], SQ[:, sq, :], True, True, ["ones", ("SQ", sq)], [("ps", nb)])
                    act(Fs(t), ps[nb][:, :], AF.Sqrt, [("ps", nb)], [("F", t)], bias=EPS, scale=1.0 / HD)
                    rcp(Fs(t), Fs(t), [("F", t)], [("F", t)])
                    stg = n % 2
                    need_f32 = rope or (prompt and not isq)
                    if not need_f32:
                        stt(STG[stg], Fs(qf), gv, Fs(t), ALU.mult, ALU.mult, [("F", qf), ("F", t), "smallT"], [("STG", stg)])
                    else:
                        stt(Fs(qf), Fs(qf), gv, Fs(t), ALU.mult, ALU.mult, [("F", qf), ("F", t), "smallT"], [("F", qf)])
                        if rope:
                            act(SQ[:, sq, :], Fs(qf), AF.Identity, [("F", qf)], [("SQ", sq)])
                            mm(ps[6][:, :], swapm[:], SQ[:, sq, :], True, True, ["swapm", ("SQ", sq)], [("ps", 6)])
                            tt(Fs(13), ps[6][:, :], Fs(RSn), ALU.mult, [("ps", 6), ("F", RSn)], [("F", 13)])
                            tt(Fs(qf), Fs(qf), Fs(RC), ALU.mult, [("F", qf), ("F", RC)], [("F", qf)])
                            tt(STG[stg], Fs(qf), Fs(13), ALU.add, [("F", qf), ("F", 13)], [("STG", stg)])
                        else:
                            act(STG[stg], Fs(qf), AF.Identity, [("F", qf)], [("STG", stg)])
                    dst, rn, hidx = (qT_s, "qT", head) if isq else (kT_s, "kT", head)
                    lo = 0
                    for (off, ln) in segs:
                        dma("sp", dst[hidx, :, off:off + ln], STG[stg][:, lo:lo + ln], [("STG", stg)],
                            sres(rn, hidx, [(off, ln)]), ("STG", stg))
                        lo += ln
                    if prompt and not isq:
                        for tb in range(4):
                            tr(ps[7][:, tb * 128:(tb + 1) * 128], Fr[:, qf, tb * 128:(tb + 1) * 128], [("F", qf)], [("ps", 7)])
                        hs = HS[n % 4]
                        cp(Fs(hs), ps[7][:, :], [("ps", 7)], [("F", hs)])
                        od, hc = (nak_d, head) if kind == "ka" else (nbk_d, head - 4)
                        for tb in range(4):
                            dma("sp", od[tb // 2, l, (tb % 2) * 128:(tb % 2 + 1) * 128, hc * 128:(hc + 1) * 128],
                                Fr[:, hs, tb * 128:(tb + 1) * 128], [("F", hs)], [("ok", kind, head, tb)], ("Fst", hs))

            def v_slot(col0, head0):
                s = wload(wv, col0, 0, 32, ("win", l, col0))
                for j in range(2):
                    n = ncnt[0]
                    ncnt[0] += 1
                    bk = n % 4
                    proj_fm(s, j, bk)
                    head = head0 + j
                    qf = 11 + n % 2
                    act(Fs(qf), ps[bk][:, :], AF.Identity, [("ps", bk)], [("F", qf)])
                    for tb in range(4):
                        tr(ps[7][:, tb * 128:(tb + 1) * 128], Fr[:, qf, tb * 128:(tb + 1) * 128], [("F", qf)], [("ps", 7)])
                    stg = n % 2
                    cp(STG[stg], ps[7][:, :], [("ps", 7)], [("STG", stg)])
                    lo = 0
                    for (off, ln) in segs:
                        g0, cnt = off // 128, ln // 128
                        dma("sp", V_s[head, :, g0:g0 + cnt, :], STG[stg][:, lo:lo + ln].rearrange("p (k d) -> p k d", d=128),
                            [("STG", stg)], sres("V", head, [(off, ln)]), ("STG", stg))
                        lo += ln
                    if prompt:
                        hs = HS[n % 4]
                        cp(Fs(hs), ps[7][:, :], [("ps", 7)], [("F", hs)])
                        od, hc = (nav_d, head) if head < 4 else (nbv_d, head - 4)
                        for tb in range(4):
                            dma("sp", od[tb // 2, l, (tb % 2) * 128:(tb % 2 + 1) * 128, hc * 128:(hc + 1) * 128],
                                Fr[:, hs, tb * 128:(tb + 1) * 128], [("F", hs)], [("ov", head, tb)], ("Fst", hs))

            def conv_slots(i):
                s = wload(wv, 7168 + i * 256, 0, 32, ("win", l, 7168 + i * 256))
                for j in range(2):
                    n = ncnt[0]
                    ncnt[0] += 1
                    bk = n % 4
                    proj_fm(s, j, bk)
                    act(Fs(XC0 + j), ps[bk][:, :], AF.Identity, [("ps", bk)], [("F", XC0 + j)])
                if full:
                    s = wload(wv, 8192 + i * 256, 0, 32, ("win", l, 8192 + i * 256))
                    for j in range(2):
                        n = ncnt[0]
                        ncnt[0] += 1
                        bk = n % 4
                        proj_fm(s, j, bk)
                        hs = HS[n % 4]
                        cp(Fs(hs), ps[bk][:, :], [("ps", bk)], [("F", hs)])
                        seg_dma_out(hs, gbT_s, 2 * i + j, segs, "gb")
                s = wload(wv, 9216 + i * 256, 0, 32, ("win", l, 9216 + i * 256))
                for j in range(2):
                    n = ncnt[0]
                    ncnt[0] += 1
                    bk = n % 4
                    proj_fm(s, j, bk)
                    hs = HS[n % 4]
                    tt(Fs(hs), ps[bk][:, :], Fs(XC0 + j), ALU.mult, [("ps", bk), ("F", XC0 + j)], [("F", hs)])
                    if not prompt:
                        tt(Fs(hs), Fs(hs), Fs(TV), ALU.mult, [("F", hs), ("F", TV)], [("F", hs)])
                    seg_dma_out(hs, uT_s, 2 * i + j, segs, "u")

            if full and MSUB & 1:
                for i in range(6):
                    qk_slot(i * 256, "qa", 2 * i)
            for i in range(2 if MSUB & 2 else 0):
                qk_slot(1536 + i * 256, "ka", 2 * i)
            for i in range(2 if MSUB & 4 else 0):
                v_slot(2048 + i * 256, 2 * i)
            if full and MSUB & 8:
                for i in range(6):
                    qk_slot(2560 + i * 256, "qb", 12 + 2 * i)
            for i in range(6 if MSUB & 16 else 0):
                qk_slot(4096 + i * 256, "kb", 4 + 2 * i)
            for i in range(6 if MSUB & 32 else 0):
                v_slot(5632 + i * 256, 4 + 2 * i)
            for i in range(4 if MSUB & 64 else 0):
                conv_slots(i)

        acnt = [0]

        def attn_tile(l, qoff, v, prompt):
            b0 = qoff // 128
            segs = [(qoff, 512)]
            if not prompt:
                dma("sp", mAt, maskA_d[b0:b0 + 4, :, :].rearrange("q p f -> p q f"), (), ["mAt"], "mAt")
                dma("sp", mBt, maskB_d[b0:b0 + 4, :, :].rearrange("q p f -> p q f"), (), ["mBt"], "mBt")
            for hd in range(24):
                isA = hd < 12
                h = hd if isA else hd - 12
                kvh = h // 3 if isA else 4 + h
                DL = 1 if isA else 3
                bf = hd % 2
                dma("sp", qTh[bf], qT_s[hd, :, qoff:qoff + 512], sres("qT", hd, segs), [("qTh", bf)], ("qTh", bf))
                if prompt:
                    kb_lo, kb_hi = 16, 19
                else:
                    kb_lo, kb_hi = max(b0 - DL, 0), min(b0 + 3 + DL, 15)
                nkb = kb_hi - kb_lo + 1
                kseg = [(kb_lo * 128, nkb * 128)]
                dma("sp", KTw[bf][:, 0:nkb * 128], kT_s[kvh, :, kb_lo * 128:(kb_hi + 1) * 128], sres("kT", kvh, kseg),
                    [("KTw", bf)], ("KTw", bf))
                dma("sp", Vw[bf][:, 0:nkb, :], V_s[kvh, :, kb_lo:kb_hi + 1, :], sres("V", kvh, kseg), [("Vw", bf)], ("Vw", bf))
                if not prompt:
                    dma("sp", cK[bf], ckT_s[kvh, :, :], [("ckT", kvh)], [("cK", bf)], ("cK", bf))
                    cvs = 4 + bf
                    csrc = cav_d[l][:, kvh * 128:(kvh + 1) * 128] if isA else cbv_d[l][:, (kvh - 4) * 128:(kvh - 3) * 128]
                    dma("sp", Fs(cvs).rearrange("p (k d) -> p k d", d=128), csrc.rearrange("(k p) d -> p k d", p=128), (),
                        [("F", cvs)], ("F", cvs))
                    act(cVb[bf].rearrange("p k d -> p (k d)"), Fs(cvs), AF.Identity, [("F", cvs)], [("cVb", bf)])
                for qi in range(4):
                    n = acnt[0]
                    acnt[0] += 1
                    pb = n % 2
                    b = b0 + qi
                    qcols = qTh[bf][:, qi * 128:(qi + 1) * 128]
                    if prompt:
                        sq_ = qi // 2
                        local = [(16 + 2 * sq_, None), (17 + 2 * sq_, None)]
                    else:
                        local = [(b + d, d) for d in range(-DL, DL + 1) if 0 <= b + d <= 15]
                    nl = len(local)
                    es_ = F2(0) if pb == 0 else F2(2)
                    esr = [("F", 0), ("F", 1)] if pb == 0 else [("F", 2), ("F", 3)]
                    for idx, (kb, d) in enumerate(local):
                        bank = pb * 3 + idx // 4
                        col = (idx % 4) * 128
                        mm(ps[bank][:, col:col + 128], KTw[bf][:, (kb - kb_lo) * 128:(kb - kb_lo + 1) * 128], qcols, True, True,
                           [("KTw", bf), ("qTh", bf)], [("ps", bank)])
                    groups = [(0, min(nl, 4))] + ([(4, nl)] if nl > 4 else [])
                    for gi, (i0, i1) in enumerate(groups):
                        bank = pb * 3 + gi
                        wd = (i1 - i0) * 128
                        if prompt:
                            act(PT[pb][:, i0 * 128:i1 * 128], ps[bank][:, 0:wd], AF.Exp, [("ps", bank)], [("PT", pb)], scale=SCALE)
                        else:
                            act(es_[:, i0 * 128:i1 * 128], ps[bank][:, 0:wd], AF.Exp, [("ps", bank)], esr, scale=SCALE)
                            mc = (local[i0][1] + DL) * 128
                            if isA:
                                tt(PT[pb][:, i0 * 128:i1 * 128], es_[:, i0 * 128:i1 * 128], mAt[:, qi, mc:mc + wd], ALU.mult,
                                   esr + ["mAt"], [("PT", pb)])
                            else:
                                tt(es_[:, i0 * 128:i1 * 128], es_[:, i0 * 128:i1 * 128], EBv[:, h, mc:mc + wd], ALU.mult,
                                   esr + ["EB"], esr)
                                tt(PT[pb][:, i0 * 128:i1 * 128], es_[:, i0 * 128:i1 * 128], mBt[:, qi, mc:mc + wd], ALU.mult,
                                   esr + ["mBt"], [("PT", pb)])
                    tiles = [(Vw[bf][:, kb - kb_lo, :], PT[pb][:, i * 128:(i + 1) * 128]) for i, (kb, d) in enumerate(local)]
                    rr = [("Vw", bf), ("PT", pb), "ones"]
                    if not prompt:
                        bank = pb * 3 + 2
                        for j in range(4):
                            mm(ps[bank][:, j * 128:(j + 1) * 128], cK[bf][:, j * 128:(j + 1) * 128], qcols, True, True,
                               [("cK", bf), ("qTh", bf)], [("ps", bank)])
                        act(PT[pb][:, 896:1408], ps[bank][:, :], AF.Exp, [("ps", bank)], [("PT", pb)], scale=SCALE)
                        tiles += [(cVb[bf][:, j, :], PT[pb][:, 896 + j * 128:896 + (j + 1) * 128]) for j in range(4)]
                        rr.append(("cVb", bf))
                    oc = (n % 4) * 128
                    for i, (vap, pap) in enumerate(tiles):
                        mm(ps[6][:, oc:oc + 128], vap, pap, i == 0, i == len(tiles) - 1, rr, [("ps", 6)])
                        mm(ps[7][:, oc:oc + 128], ones[:], pap, i == 0, i == len(tiles) - 1, rr, [("ps", 7)])
                    t = TT_[n % 2]
                    if isA:
                        ts2(Fr[:, t, 0:128], ps[7][:, oc:oc + 128], sinkE[:, h:h + 1], 1.0, ALU.add, ALU.mult,
                            [("ps", 7), "sinkE"], [("F", t)])
                        rcp(Fr[:, t, 0:128], Fr[:, t, 0:128], [("F", t)], [("F", t)])
                    else:
                        rcp(Fr[:, t, 0:128], ps[7][:, oc:oc + 128], [("ps", 7)], [("F", t)])
                    tt(xT[:, hd, qi * 128:(qi + 1) * 128], ps[6][:, oc:oc + 128], Fr[:, t, 0:128], ALU.mult,
                       [("ps", 6), ("F", t)], [("xT", hd)])
            U = F2(10)
            for ch in range(8):
                w0, w1, w2 = [smallT[:, R_CONVW + l * 24 + j * 8 + ch:R_CONVW + l * 24 + j * 8 + ch + 1] for j in range(3)]
                cb = smallT[:, R_CONVB + l * 8 + ch:R_CONVB + l * 8 + ch + 1]
                ur = [("F", 10), ("F", 11)]
                dma("sp", Fs(12), gbT_s[ch, :, qoff:qoff + 512], sres("gb", ch, segs), [("F", 12)], ("F", 12))
                if not prompt:
                    dma("sp", U[:, 0:514], uT_s[ch, :, qoff - 1:qoff + 513], sres("u", ch, [(qoff - 1, 514)]), ur, ("F", 10))
                    parts = [(0, 512)]
                else:
                    parts = [(0, 256), (256, 256)]
                for (p0, pl) in parts:
                    if prompt:
                        mset(U[:, 0:1], 0.0, ur)
                        mset(U[:, pl + 1:pl + 2], 0.0, ur)
                        dma("sp", U[:, 1:pl + 1], uT_s[ch, :, qoff + p0:qoff + p0 + pl], sres("u", ch, [(qoff + p0, pl)]), ur,
                            ("F", 10))
                    y = Fr[:, 13, p0:p0 + pl]
                    ts2(y, U[:, 1:pl + 1], w1, cb, ALU.mult, ALU.add, ur + ["smallT"], [("F", 13)])
                    stt(y, U[:, 0:pl], w0, y, ALU.mult, ALU.add, ur + ["smallT", ("F", 13)], [("F", 13)])
                    stt(y, U[:, 2:pl + 2], w2, y, ALU.mult, ALU.add, ur + ["smallT", ("F", 13)], [("F", 13)])
                    tt(xT[:, 24 + ch, p0:p0 + pl], y, Fr[:, 12, p0:p0 + pl], ALU.mult, [("F", 13), ("F", 12)], [("xT", 24 + ch)])
            wvo = wview(wout_d[l])
            for mi in range(16):
                s = wload(wvo, mi * 256, 0, 32, ("wout", l, mi))
                for mb in range(2):
                    m = 2 * mi + mb
                    bk = m % 4
                    for c in range(NCH):
                        mm(ps[bk][:, :], WS[s][:, c, mb * 128:(mb + 1) * 128], xT[:, c, :], c == 0, c == NCH - 1,
                           [("W", s), ("xT", c)], [("ps", bk)])
                    sl = 6 + m % 2
                    load_h(m, sl, segs)
                    stt(Fs(sl), ps[bk][:, :], modG(v, 1, m), Fs(sl), ALU.mult, ALU.add, [("ps", bk), "DER", ("F", sl)], [("F", sl)])
                    store_h(m, sl, segs)

        PSEG = [(POFF, 512)]

        def sseg(off):
            return [(off, 512)]

        if stage != 99:
            mset(MODT[:, :, :], 0.0, ["MODT"])
            for v in range(2):
                for s3 in range(3):
                    mset(MODT[:, v, (3 * s3 + 2) * 32:(3 * s3 + 3) * 32], 1.0, ["MODT"])
                    g0 = R_NORM + s3 * 32
                    ts2(DER[:, v, s3, 0, :], MODT[:, v, (3 * s3 + 1) * 32:(3 * s3 + 2) * 32], 1.0, 1.0, ALU.add, ALU.mult,
                        ["MODT"], ["DER"])
                    tt(DER[:, v, s3, 0, :], DER[:, v, s3, 0, :], smallT[:, g0:g0 + 32], ALU.mult, ["DER", "smallT"], ["DER"])
                    ts2(DER[:, v, s3, 1, :], MODT[:, v, (3 * s3 + 2) * 32:(3 * s3 + 3) * 32], 1.0, 0.0,
                        ALU.mult, ALU.add, ["MODT"], ["DER"])
            arena_fence()
            if stage not in (20, 21, 30):
                mixer_prep(0)
            if stage not in (20, 22, 30):
                mixer_in_tile(0, PSEG, 0, True, True)
            if stage >= 11 and stage < 20:
                mixer_in_tile(0, sseg(256), 1, True, False)
                mixer_in_tile(0, sseg(768), 1, True, False)
                mixer_in_tile(0, [(0, 256), (1792, 256)], 1, False, False)
            if stage not in (21, 22, 30):
                attn_tile(0, POFF, 0, True)
            if stage >= 11 and stage < 20:
                attn_tile(0, 256, 1, False)
        for l in (range(NL) if stage == 99 else ()):
            modulation(l)
            w1 = [0, 512, 1024, 1536] if l == 0 else [256, 768, 1280]
            wq = [256, 768, 1280] if l == 0 else [512, 1024]
            kvo = [(0, 256), (1792, 256)] if l == 0 else [(256, 256), (1536, 256)]
            ffn_tile(l, 0, PSEG, 0)
            for w in w1:
                ffn_tile(l, 0, sseg(w), 1)
            arena_fence()
            mixer_prep(l)
            mixer_in_tile(l, PSEG, 0, True, True)
            for w in wq:
                mixer_in_tile(l, sseg(w), 1, True, False)
            mixer_in_tile(l, kvo, 1, False, False)
            attn_tile(l, POFF, 0, True)
            for w in wq:
                attn_tile(l, w, 1, False)
            arena_fence()
            ffn_tile(l, 1, PSEG, 0)
            for w in wq:
                ffn_tile(l, 1, sseg(w), 1)

        for (off, dst, r0) in ((512, ys_d, 0), (1024, ys_d, 512), (POFF, yp_d, 0)):
            for cg in range(8):
                for cc in range(4):
                    load_h(cg * 4 + cc, HL[cc], [(off, 512)])
                for tb in range(4):
                    bk = 4 + tb % 4
                    for cc in range(4):
                        tr(ps[bk][:, cc * 128:(cc + 1) * 128], Fr[:, HL[cc], tb * 128:(tb + 1) * 128],
                           [("F", HL[cc])], [("ps", bk)])
                    hs = HS[tb]
                    cp(Fs(hs), ps[bk][:, :], [("ps", bk)], [("F", hs)])
                    dma("sp", dst[r0 + tb * 128:r0 + (tb + 1) * 128, cg * 512:(cg + 1) * 512], Fs(hs),
                        [("F", hs)], [("y", off, tb, cg)], ("Fst", hs))

        S.emit(nc, block, es)
    return nc


def _small_table(c_ctx, c_mine, mod_b, norm_ffn1, norm_mix, norm_ffn2, conv_w, conv_b, a_q_norm, a_k_norm, b_q_norm, b_k_norm):
    sm = np.zeros((R_TOT, 128), np.float32)
    sm[R_CVEC:R_CVEC + 32] = c_ctx.reshape(32, 128)
    sm[R_CVEC + 32:R_CVEC + 64] = c_mine.reshape(32, 128)
    for l in range(NL):
        sm[R_MODB + l * 288:R_MODB + (l + 1) * 288] = mod_b[l].reshape(288, 128)
        for s3, nm in enumerate((norm_ffn1, norm_mix, norm_ffn2)):
            sm[R_NORM + (l * 3 + s3) * 32:R_NORM + (l * 3 + s3 + 1) * 32] = nm[l].reshape(32, 128)
        sm[R_CONVW + l * 24:R_CONVW + (l + 1) * 24] = conv_w[l].reshape(24, 128)
        sm[R_CONVB + l * 8:R_CONVB + (l + 1) * 8] = conv_b[l].reshape(8, 128)
        for j, q in enumerate((a_q_norm, a_k_norm, b_q_norm, b_k_norm)):
            sm[R_QKN + l * 4 + j] = q[l]
    return sm


def _pos_consts(qd):
    base_row = 16 * qd - 8
    t = np.arange(2048)
    tg = base_row * 64 + t
    row = np.floor_divide(tg, 64)
    col = np.mod(tg, 64)
    valid_t = (row >= 0) & (row < 64)
    inv = (10000.0 ** (-np.arange(32, dtype=np.float32) * 2.0 / 64.0)).astype(np.float32)
    d = np.arange(128)
    axis, within = d // 64, d % 64
    pair = within % 32
    sign = np.where(within < 32, -1.0, 1.0).astype(np.float32)
    pos = np.where(axis[:, None] == 0, row[None, :], col[None, :]).astype(np.float32)
    ang = (pos * inv[pair][:, None]).astype(np.float32)
    ropeC = np.cos(ang).astype(np.float32)
    ropeS = (np.sin(ang).astype(np.float32) * sign[:, None]).astype(np.float32)
    m = np.arange(128)
    partner = np.where((m % 64) < 32, m + 32, m - 32)
    swapm = np.zeros((128, 128), np.float32)
    swapm[partner, m] = 1.0
    k = np.arange(128)[:, None]
    q = np.arange(128)[None, :]
    maskA = np.zeros((16, 128, 3, 128), np.float32)
    maskB = np.zeros((16, 128, 7, 128), np.float32)
    for bq in range(16):
        qt = bq * 128 + q
        qrow = base_row + qt // 64
        qc = qt % 64
        rs = np.clip(qrow - 4, 0, 56)
        cs = np.clip(qc - 8, 0, 48)
        for dl in range(-3, 4):
            kb = bq + dl
            if kb < 0 or kb > 15:
                continue
            kt = kb * 128 + k
            krow = base_row + kt // 64
            kc = kt % 64
            kval = (krow >= 0) & (krow < 64)
            if -1 <= dl <= 1:
                maskA[bq, :, dl + 1, :] = (np.abs(qt - kt) <= 128) & kval
            maskB[bq, :, dl + 3, :] = kval & (krow >= rs) & (krow < rs + 8) & (kc >= cs) & (kc < cs + 16)
    bf = ml_dtypes.bfloat16
    return {"ropeC": ropeC, "ropeS": ropeS, "swapm": swapm.astype(bf),
            "tokvalid": np.ascontiguousarray(np.broadcast_to(valid_t.astype(np.float32)[None, :], (128, 2048))),
            "maskA": maskA.reshape(16, 128, 384).astype(bf), "maskB": maskB.reshape(16, 128, 896).astype(bf)}


def _core_inputs(i, inp, pc, shared):
    b, qd = i // 4, i % 4
    r0 = 16 * qd
    xs = np.zeros((2048, D), np.float32)
    lo, hi = max(r0 - 8, 0), min(r0 + 24, 64)
    xs[(lo - (r0 - 8)) * 64:(hi - (r0 - 8)) * 64] = inp["x_sample"][b, lo * 64:hi * 64]
    xp = np.ascontiguousarray(inp["x_prompt"][2 * i:2 * i + 2].reshape(512, D))
    sm = _small_table(inp["c_ctx"], inp["c"][b], inp["mod_b"], inp["norm_ffn1"], inp["norm_mix"], inp["norm_ffn2"],
                      inp["conv_w"], inp["conv_b"], inp["a_q_norm"], inp["a_k_norm"], inp["b_q_norm"], inp["b_k_norm"])
    m = {"xs": xs, "xp": xp, "small": sm, "ident": np.eye(128, dtype=np.float32),
         "cak": np.ascontiguousarray(inp["cache_a_k"][b].reshape(NL, 512, 512)),
         "cav": np.ascontiguousarray(inp["cache_a_v"][b].reshape(NL, 512, 512)),
         "cbk": np.ascontiguousarray(inp["cache_b_k"][b].reshape(NL, 512, 1536)),
         "cbv": np.ascontiguousarray(inp["cache_b_v"][b].reshape(NL, 512, 1536))}
    m.update(pc[qd])
    m.update(shared)
    return m


def run(inp, stage=99, trace=False):
    inp = {k: np.asarray(v) for k, v in inp.items()}
    nc = build(stage)
    shared = {k: np.ascontiguousarray(inp[k]) for k in ("mod_w", "ffn1_w13", "ffn1_w2", "ffn2_w13", "ffn2_w2", "w_in", "w_out")
              if k in DECL}
    kc = np.arange(64)[:, None]
    qc = np.arange(64)[None, :]
    dc = np.clip(kc - qc, -15, 15) + 15
    rp = inp["b_rpb"]
    shared["Tz"] = np.ascontiguousarray(np.transpose(rp[:, :, :, dc], (0, 2, 3, 1, 4)))
    shared["sinkb"] = np.ascontiguousarray(np.broadcast_to(inp["a_sink"][:, None, :], (NL, 128, 12)))
    pc = [_pos_consts(qd) for qd in range(4)]
    in_maps = [_core_inputs(i, inp, pc, shared) for i in range(8)]
    in_maps = [{k: v for k, v in m.items() if k in DECL} for m in in_maps]
    return run_bass_kernel_spmd(nc, in_maps, core_ids=list(range(8)), trace=trace)


def kernel(**inputs):
    r = run(inputs).results
    yp = np.concatenate([r[i]["yp"].reshape(2, 256, D) for i in range(8)], axis=0)
    ys = np.stack([np.concatenate([r[4 * b + q]["ys"] for q in range(4)], axis=0) for b in range(2)], axis=0)
    nak = np.concatenate([r[i]["nak"].reshape(2, NL, 256, 4, 128) for i in range(8)], axis=0)
    nav = np.concatenate([r[i]["nav"].reshape(2, NL, 256, 4, 128) for i in range(8)], axis=0)
    nbk = np.concatenate([r[i]["nbk"].reshape(2, NL, 256, 12, 128) for i in range(8)], axis=0)
    nbv = np.concatenate([r[i]["nbv"].reshape(2, NL, 256, 12, 128) for i in range(8)], axis=0)
    return yp, ys, nak, nav, nbk, nbv
```

```python
import numpy as np
from contextlib import ExitStack
import ml_dtypes
import concourse.bass as bass
import concourse.mybir as mybir
from concourse.bass_utils import run_bass_kernel_spmd

F32 = mybir.dt.float32
BF16 = mybir.dt.bfloat16
AF = mybir.ActivationFunctionType
ALU = mybir.AluOpType

D = 4096
DFF = 11008
NL = 2
NCH = 32
GCH = 86
T = 512
NTOK = 2560
POFF = 2048
EPS = 1e-6
NQA, NKVA, NHB, HD = 12, 4, 12, 128
INW = 10240
NMOD = 9
SCALE = HD ** -0.5
COMPUTE = ("pe", "act", "dve")
DMAQ = ("sp", "pool")

R_CVEC, R_MODB, R_NORM, R_CONVW, R_CONVB, R_QKN, R_TOT = 0, 64, 640, 832, 880, 896, 1024


class _Op:
    __slots__ = ("eng", "fn", "deps", "sig", "cnt", "key", "val")


class Sched:
    def __init__(self):
        self.ops = {e: [] for e in COMPUTE + DMAQ}
        self.lastw = {}
        self.rd = {}
        self.lastdma = {}
        self.dmacnt = {}
        self.keyq = {}

    def add(self, eng, fn, R=(), W=(), key=None):
        op = _Op()
        op.eng, op.fn, op.sig, op.key, op.cnt, op.val = eng, fn, False, key, 0, 0
        deps = {}

        def dep(o):
            if o is None or (o.eng == "pe" and eng == "pe"):
                return
            deps[id(o)] = o

        for r in R:
            dep(self.lastw.get(r))
        for w in W:
            dep(self.lastw.get(w))
            for o in self.rd.get(w, ()):
                dep(o)
        if key is not None:
            dep(self.lastdma.get(key))
            self.lastdma[key] = op
            self.dmacnt[key] = self.dmacnt.get(key, 0) + 1
            op.val = 16 * self.dmacnt[key]
            self.keyq[key] = eng
        op.deps = list(deps.values())
        for o in op.deps:
            if o.key is None:
                o.sig = True
        for w in W:
            self.lastw[w] = op
            self.rd[w] = []
        for r in R:
            lst = self.rd.setdefault(r, [])
            if eng in COMPUTE:
                lst[:] = [o for o in lst if o.eng != eng]
            lst.append(op)
        self.ops[eng].append(op)

    def barrier(self):
        keys = set(self.lastw) | set(self.rd)
        self._bar = getattr(self, "_bar", 0) + 1
        return keys

    def emit(self, nc, block, es):
        prog = {e: es.enter_context(nc.semaphore("pg_" + e)) for e in COMPUTE}
        dsem = {k: es.enter_context(nc.semaphore("dq%d" % i)) for i, k in enumerate(self.dmacnt)}
        for e in COMPUTE:
            c = 0
            for op in self.ops[e]:
                if op.sig:
                    c += 1
                op.cnt = c

        def run(e, h):
            waited = {}
            for op in self.ops[e]:
                for o in op.deps:
                    if o.key is not None:
                        sem, v = dsem[o.key], o.val
                    else:
                        sem, v = prog[o.eng], o.cnt
                    if waited.get(id(sem), 0) < v:
                        h.wait_ge(sem, v)
                        waited[id(sem)] = v
                ins = op.fn(h)
                if op.key is not None:
                    ins.then_inc(dsem[op.key], 16)
                elif op.sig:
                    ins.then_inc(prog[e], 1)
            if e in DMAQ:
                for k, q in self.keyq.items():
                    if q == e and waited.get(id(dsem[k]), 0) < 16 * self.dmacnt[k]:
                        h.wait_ge(dsem[k], 16 * self.dmacnt[k])

        @block.tensor
        def _(h):
            run("pe", h)

        @block.scalar
        def _(h):
            run("act", h)

        @block.vector
        def _(h):
            run("dve", h)

        @block.sync
        def _(h):
            run("sp", h)

        @block.gpsimd
        def _(h):
            run("pool", h)


DECL = []
import os as _os
MSUB = int(_os.environ.get('MSUB', '127'))


def hres(c, segs):
    out = []
    for off, ln in segs:
        for j in range(off // 256, (off + ln - 1) // 256 + 1):
            out.append(("hT", c, j))
    return out


def sres(name, idx, segs):
    out = []
    for off, ln in segs:
        for j in range(off // 256, (off + ln - 1) // 256 + 1):
            out.append((name, idx, j))
    return out


def build(stage=99):
    nc = bass.Bass("TRN2", target_bir_lowering=False)
    S = Sched()

    DECL.clear()

    def din(name, shape, dt=F32):
        DECL.append(name)
        return nc.dram_tensor(name, list(shape), dt, kind="ExternalInput").ap()

    def dout(name, shape):
        return nc.dram_tensor(name, list(shape), F32, kind="ExternalOutput").ap()

    xs_d = din("xs", [2048, D])
    xp_d = din("xp", [512, D])
    small_d = din("small", [R_TOT, 128])
    ident_d = din("ident", [128, 128])
    swap_d = din("swapm", [128, 128], BF16)
    ropeC_d = din("ropeC", [128, 2048])
    ropeS_d = din("ropeS", [128, 2048])
    tokv_d = din("tokvalid", [128, 2048])
    maskA_d = din("maskA", [16, 128, 384], BF16)
    maskB_d = din("maskB", [16, 128, 896], BF16)
    Tz_d = din("Tz", [NL, 15, 64, 12, 64])
    sinkb_d = din("sinkb", [NL, 128, 12])
    cak_d = din("cak", [NL, 512, 512])
    cav_d = din("cav", [NL, 512, 512])
    cbk_d = din("cbk", [NL, 512, 1536])
    cbv_d = din("cbv", [NL, 512, 1536])
    if stage == 99:
        modw_d = din("mod_w", [NL, D, NMOD * D])
        w13_d = [din("ffn1_w13", [NL, D, 2 * DFF]), din("ffn2_w13", [NL, D, 2 * DFF])]
        w2_d = [din("ffn1_w2", [NL, DFF, D]), din("ffn2_w2", [NL, DFF, D])]
    win_d = din("w_in", [NL, D, INW])
    wout_d = din("w_out", [NL, D, D])
    ys_d = dout("ys", [1024, D])
    yp_d = dout("yp", [512, D])
    nak_d = dout("nak", [2, NL, 256, 512])
    nav_d = dout("nav", [2, NL, 256, 512])
    nbk_d = dout("nbk", [2, NL, 256, 1536])
    nbv_d = dout("nbv", [2, NL, 256, 1536])
    hT = nc.dram_tensor("hT", [NCH, 128, NTOK], F32).ap()
    qT_s = nc.dram_tensor("qT_s", [24, 128, NTOK], BF16).ap()
    kT_s = nc.dram_tensor("kT_s", [16, 128, NTOK], BF16).ap()
    V_s = nc.dram_tensor("V_s", [16, 128, 20, 128], BF16).ap()
    uT_s = nc.dram_tensor("uT_s", [8, 128, NTOK], F32).ap()
    gbT_s = nc.dram_tensor("gbT_s", [8, 128, NTOK], F32).ap()
    ckT_s = nc.dram_tensor("ckT_s", [16, 128, 512], BF16).ap()
    modraw_d = nc.dram_tensor("modraw", [128, 2, 288], F32).ap()
    wc_t = [nc.dram_tensor("wcache%d" % i, [120, 128, 8192], BF16).ap() for i in range(6)]

    es = ExitStack()
    with es:
        E = es.enter_context
        WS = [E(nc.sbuf_tensor("ws%d" % i, [128, 32, 256], BF16)) for i in range(3)]
        xT = E(nc.sbuf_tensor("xT", [128, NCH, T], BF16))
        AR = E(nc.sbuf_tensor("arena", [128, GCH * T], BF16))
        Fr = E(nc.sbuf_tensor("fring", [128, 14, T], F32))
        SQ = E(nc.sbuf_tensor("sq", [128, 2, T], BF16))
        smallT = E(nc.sbuf_tensor("smallT", [128, R_TOT], F32))
        ident = E(nc.sbuf_tensor("identS", [128, 128], F32))
        ones = E(nc.sbuf_tensor("ones", [128, 128], BF16))
        swapm = E(nc.sbuf_tensor("swapS", [128, 128], BF16))
        scT = E(nc.sbuf_tensor("scT", [128, 32, 2], BF16))
        _m = E(nc.sbuf_tensor("modT", [128, 2, 288], F32))
        _d = E(nc.sbuf_tensor("der", [128, 2, 3, 2, 32], F32))
        MODTs, DERs = [_m, _m], [_d, _d]
        modst = E(nc.sbuf_tensor("modst", [128, 8, 4], F32))
        curl = [0]
        sinkE = E(nc.sbuf_tensor("sinkE", [128, 12], F32))
        fsc = E(nc.sbuf_tensor("fsc", [128, 2], F32))
        ps = [E(nc.psum_tensor("ps%d" % i, [128, T], F32)) for i in range(8)]
        block = E(nc.Block())

        def gT(n):
            return AR[:, n * T:(n + 1) * T]

        def Fs(i):
            return Fr[:, i, :]

        def F2(i):
            return Fr[:, i:i + 2, :].rearrange("p a b -> p (a b)")

        EBv = AR[:, 0:10752].rearrange("p (h f) -> p h f", f=896)
        mAt = AR[:, 10752:12288].rearrange("p (q f) -> p q f", f=384)
        mBt = AR[:, 12288:15872].rearrange("p (q f) -> p q f", f=896)
        KTw = [AR[:, 15872 + i * 1280:15872 + (i + 1) * 1280] for i in range(2)]
        Vw = [AR[:, 18432 + i * 1280:18432 + (i + 1) * 1280].rearrange("p (k d) -> p k d", d=128) for i in range(2)]
        cK = [AR[:, 20992 + i * 512:20992 + (i + 1) * 512] for i in range(2)]
        cVb = [AR[:, 22016 + i * 512:22016 + (i + 1) * 512].rearrange("p (k d) -> p k d", d=128) for i in range(2)]
        qTh = [AR[:, 23040 + i * 512:23040 + (i + 1) * 512] for i in range(2)]
        PT = [AR[:, 24064 + i * 1408:24064 + (i + 1) * 1408] for i in range(2)]
        STG = [AR[:, 26880 + i * 512:26880 + (i + 1) * 512] for i in range(2)]
        VST = [AR[:, 27904 + i * 256:27904 + (i + 1) * 256] for i in range(2)]
        ATT_KEYS = (["EB", "mAt", "mBt"] + [(n, i) for n in ("KTw", "Vw", "cK", "cVb", "qTh", "PT", "STG", "VST")
                                            for i in range(2)])

        HL = [0, 1, 2, 3]
        HS = [4, 5, 6, 7]
        TT_ = [8, 9]
        RS = 10

        def mm(out, lhsT, rhs, start, stop, R, W):
            S.add("pe", lambda h: h.matmul(out, lhsT, rhs, start=start, stop=stop), R, W)

        def tr(out, in_, R, W):
            S.add("pe", lambda h: h.transpose(out, in_, ident[:]), list(R) + ["ident"], W)

        def act(out, in_, func, R, W, bias=0.0, scale=1.0):
            S.add("act", lambda h: h.activation(out, in_, func, bias=bias, scale=scale), R, W)

        def tt(out, a, b, op, R, W):
            S.add("dve", lambda h: h.tensor_tensor(out, a, b, op), R, W)

        def ts2(out, a, s1, s2, op0, op1, R, W):
            S.add("dve", lambda h: h.tensor_scalar(out, a, s1, s2, op0, op1), R, W)

        def stt(out, a, sc, b, op0, op1, R, W):
            S.add("dve", lambda h: h.scalar_tensor_tensor(out, a, sc, b, op0, op1), R, W)

        def cp(out, in_, R, W):
            S.add("dve", lambda h: h.tensor_copy(out, in_), R, W)

        def rcp(out, in_, R, W):
            S.add("dve", lambda h: h.reciprocal(out, in_), R, W)

        def mset(out, val, W):
            S.add("dve", lambda h: h.memset(out, val), (), W)

        def dma(q, out, in_, R, W, key):
            S.add(q, lambda h: h.dma_start(out=out, in_=in_), R, W, key=key)

        def arena_fence():
            mset(fsc[:, 0:1], 0.0, [("g", n) for n in range(GCH)] + ATT_KEYS + ["fsc"])

        wcnt = [0]

        wcache = {}

        def wload(view, col0, c0, c1, ckey=None):
            s = wcnt[0] % 3
            wcnt[0] += 1
            n = (c1 - c0) * 256
            flat = WS[s][:, :, :].rearrange("p c n -> p (c n)")
            if ckey is not None and ckey in wcache:
                sid = wcache[ckey]
                dma("pool", flat[:, 0:n], wc_t[sid // 120][sid % 120, :, 0:n], [("wc", sid)], [("W", s)], ("W", s))
            else:
                dma("pool", WS[s][:, 0:c1 - c0, :], view[:, c0:c1, col0:col0 + 256], (), [("W", s)], ("W", s))
                if ckey is not None:
                    sid = len(wcache)
                    wcache[ckey] = sid
                    dma("sp", wc_t[sid // 120][sid % 120, :, 0:n], flat[:, 0:n], [("W", s)], [("wc", sid)], ("wcst", s))
            return s

        def wview(w):
            return w.rearrange("(c p) n -> p c n", p=128)

        def seg_dma_in(slot, src3, idx, segs, rname):
            lo = 0
            for (off, ln) in segs:
                dma("sp", Fr[:, slot, lo:lo + ln], src3[idx, :, off:off + ln], sres(rname, idx, [(off, ln)]),
                    [("F", slot)], ("F", slot))
                lo += ln

        def seg_dma_out(slot, dst3, idx, segs, rname):
            lo = 0
            for (off, ln) in segs:
                dma("sp", dst3[idx, :, off:off + ln], Fr[:, slot, lo:lo + ln], [("F", slot)],
                    sres(rname, idx, [(off, ln)]), ("Fst", slot))
                lo += ln

        def load_h(c, slot, segs):
            seg_dma_in(slot, hT, c, segs, "hT")

        def store_h(c, slot, segs):
            seg_dma_out(slot, hT, c, segs, "hT")

        dma("sp", ident[:], ident_d[:, :], (), ["ident"], "ident")
        dma("sp", swapm[:], swap_d[:, :], (), ["swapm"], "swapm")
        mset(ones[:], 1.0, ["ones"])
        for b in range(R_TOT // 128):
            dma("sp", Fr[:, HL[b % 4], 0:128], small_d[b * 128:(b + 1) * 128, :], (), [("F", HL[b % 4])], ("F", HL[b % 4]))
            tr(ps[b // 4][:, (b % 4) * 128:(b % 4 + 1) * 128], Fr[:, HL[b % 4], 0:128], [("F", HL[b % 4])], [("ps", b // 4)])
            if b % 4 == 3:
                cp(smallT[:, (b // 4) * 512:(b // 4 + 1) * 512], ps[b // 4][:, :], [("ps", b // 4)], ["smallT"])
        for v in range(2):
            act(scT[:, :, v], smallT[:, R_CVEC + v * 32:R_CVEC + (v + 1) * 32], AF.Silu, ["smallT"], ["scT"])

        for w in range(5):
            src = xs_d if w < 4 else xp_d
            r0 = w * 512 if w < 4 else 0
            off = w * 512
            for cg in range(8):
                for tb in range(4):
                    dma("sp", Fs(HL[tb]), src[r0 + tb * 128:r0 + (tb + 1) * 128, cg * 512:(cg + 1) * 512], (),
                        [("F", HL[tb])], ("F", HL[tb]))
                for cc in range(4):
                    c = cg * 4 + cc
                    bk = 4 + c % 4
                    for tb in range(4):
                        tr(ps[bk][:, tb * 128:(tb + 1) * 128], Fr[:, HL[tb], cc * 128:(cc + 1) * 128],
                           [("F", HL[tb])], [("ps", bk)])
                    hs = HS[c % 4]
                    cp(Fs(hs), ps[bk][:, :], [("ps", bk)], [("F", hs)])
                    store_h(c, hs, [(off, 512)])

        def mod_slot(l, si):
            mv = wview(modw_d[l])
            s = wload(mv, si * 256, 0, 32)
            bank = si % 2
            for j in range(2):
                for c in range(NCH):
                    mm(ps[bank][:, 2 * j:2 * j + 2], WS[s][:, c, j * 128:(j + 1) * 128], scT[:, c, :],
                       c == 0, c == NCH - 1, [("W", s), "scT"], [("ps", bank)])
            k = si % 8
            stv = modst[:, k, :].rearrange("p (v b) -> p v b", b=2)
            cp(stv, ps[bank][:, 0:4].rearrange("p (b v) -> p v b", v=2), [("ps", bank)], [("modst", k)])
            dma("sp", modraw_d[:, :, 2 * si:2 * si + 2], stv, [("modst", k)], [("modraw", si)], ("modst", k))

        def mod_finish(l):
            MODT, DER = MODTs[l], DERs[l]
            mb = R_MODB + l * 288
            dma("sp", MODT[:, :, :], modraw_d[:, :, :], [("modraw", si) for si in range(144)], ["MODT"], "modld")
            for v in range(2):
                tt(MODT[:, v, :], MODT[:, v, :], smallT[:, mb:mb + 288], ALU.add, ["MODT", "smallT"], ["MODT"])
            for v in range(2):
                for s3 in range(3):
                    g0 = R_NORM + (l * 3 + s3) * 32
                    ts2(DER[:, v, s3, 0, :], MODT[:, v, (3 * s3 + 1) * 32:(3 * s3 + 2) * 32], 1.0, 1.0, ALU.add, ALU.mult,
                        ["MODT"], ["DER"])
                    tt(DER[:, v, s3, 0, :], DER[:, v, s3, 0, :], smallT[:, g0:g0 + 32], ALU.mult, ["DER", "smallT"], ["DER"])
                    ts2(DER[:, v, s3, 1, :], MODT[:, v, (3 * s3 + 2) * 32:(3 * s3 + 3) * 32], 0.5 if s3 != 1 else 1.0, 0.0,
                        ALU.mult, ALU.add, ["MODT"], ["DER"])

        def modulation(l):
            for si in range(144):
                mod_slot(l, si)
            mod_finish(l)

        def modA(v, s3, c):
            return DERs[curl[0]][:, v, s3, 0, c:c + 1]

        def modG(v, s3, c):
            return DERs[curl[0]][:, v, s3, 1, c:c + 1]

        def modB(v, s3, c):
            return MODTs[curl[0]][:, v, 3 * s3 * 32 + c:3 * s3 * 32 + c + 1]

        def norm_tile(segs, v, s3):
            NB = 7
            for c in range(NCH):
                sl = HL[c % 4]
                load_h(c, sl, segs)
                act(SQ[:, c % 2, :], Fs(sl), AF.Square, [("F", sl)], [("SQ", c % 2)])
                mm(ps[NB][:, :], ones[:], SQ[:, c % 2, :], c == 0, c == NCH - 1, ["ones", ("SQ", c % 2)], [("ps", NB)])
            act(Fs(TT_[0]), ps[NB][:, :], AF.Sqrt, [("ps", NB)], [("F", TT_[0])], bias=EPS, scale=1.0 / D)
            rcp(Fs(RS), Fs(TT_[0]), [("F", TT_[0])], [("F", RS)])
            for c in range(NCH):
                sl = HL[c % 4]
                load_h(c, sl, segs)
                t = TT_[c % 2]
                tt(Fs(t), Fs(sl), Fs(RS), ALU.mult, [("F", sl), ("F", RS)], [("F", t)])
                act(xT[:, c, :], Fs(t), AF.Identity, [("F", t), "DER", "MODT"], [("xT", c)],
                    bias=modB(v, s3, c), scale=modA(v, s3, c))

        def ffn_tile(l, which, segs, v, hook=None):
            s3 = 0 if which == 0 else 2
            norm_tile(segs, v, s3)
            wv13 = wview(w13_d[which][l])
            wv2 = wview(w2_d[which][l])
            for jj in range(43):
                sa = wload(wv13, jj * 256, 0, 32, ("w13a", which, l, jj))
                sb = wload(wv13, DFF + jj * 256, 0, 32, ("w13b", which, l, jj))
                for sub in range(2):
                    n = 2 * jj + sub
                    pa, pb = n % 2, 2 + n % 2
                    for c in range(NCH):
                        mm(ps[pa][:, :], WS[sa][:, c, sub * 128:(sub + 1) * 128], xT[:, c, :], c == 0, c == NCH - 1,
                           [("W", sa), ("xT", c)], [("ps", pa)])
                    for c in range(NCH):
                        mm(ps[pb][:, :], WS[sb][:, c, sub * 128:(sub + 1) * 128], xT[:, c, :], c == 0, c == NCH - 1,
                           [("W", sb), ("xT", c)], [("ps", pb)])
                    t = TT_[n % 2]
                    act(Fs(t), ps[pa][:, :], AF.Silu, [("ps", pa)], [("F", t)])
                    tt(gT(n), Fs(t), ps[pb][:, :], ALU.mult, [("F", t), ("ps", pb)], [("g", n)])
            for mi in range(16):
                for kp in range(3):
                    c0, c1 = 32 * kp, min(32 * kp + 32, GCH)
                    s = wload(wv2, mi * 256, c0, c1, ("w2", which, l, mi, kp))
                    for mb in range(2):
                        m = 2 * mi + mb
                        bk = 4 + m % 4
                        for ci in range(c1 - c0):
                            c = c0 + ci
                            mm(ps[bk][:, :], WS[s][:, ci, mb * 128:(mb + 1) * 128], gT(c), c == 0, c == GCH - 1,
                               [("W", s), ("g", c)], [("ps", bk)])
                for mb in range(2):
                    m = 2 * mi + mb
                    bk = 4 + m % 4
                    sl, hs = HL[m % 4], HS[m % 4]
                    load_h(m, sl, segs)
                    stt(Fs(hs), ps[bk][:, :], modG(v, s3, m), Fs(sl), ALU.mult, ALU.add,
                        [("ps", bk), "DER", ("F", sl)], [("F", hs)])
                    store_h(m, hs, segs)
                if hook is not None:
                    hook(mi)

        def mixer_prep(l):
            for kvh in range(16):
                src = cak_d[l][:, kvh * 128:(kvh + 1) * 128] if kvh < 4 else cbk_d[l][:, (kvh - 4) * 128:(kvh - 3) * 128]
                bk = kvh % 2
                for tb in range(4):
                    dma("sp", Fr[:, HL[tb], 0:128], src[tb * 128:(tb + 1) * 128, :], (), [("F", HL[tb])], ("F", HL[tb]))
                    tr(ps[bk][:, tb * 128:(tb + 1) * 128], Fr[:, HL[tb], 0:128], [("F", HL[tb])], [("ps", bk)])
                cp(STG[kvh % 2], ps[bk][:, :], [("ps", bk)], [("STG", kvh % 2)])
                dma("sp", ckT_s[kvh, :, :], STG[kvh % 2], [("STG", kvh % 2)], [("ckT", kvh)], ("STG", kvh % 2))
            EBraw = Fr[:, 4:7, :].rearrange("p a b -> p (a b)").rearrange("p (h q) -> p h q", q=128)
            for di, dl in enumerate(range(-3, 4)):
                for krl in range(2):
                    for qrl in range(2):
                        dr = 2 * dl + krl - qrl + 7
                        dma("sp", EBraw[krl * 64:(krl + 1) * 64, :, qrl * 64:(qrl + 1) * 64], Tz_d[l, dr, :, :, :], (),
                            [("F", 4), ("F", 5), ("F", 6)], ("F", 4))
                act(EBv[:, :, di * 128:(di + 1) * 128], EBraw, AF.Exp, [("F", 4), ("F", 5), ("F", 6)], ["EB"])
            dma("sp", Fr[:, 8, 0:12], sinkb_d[l, :, :], (), [("F", 8)], ("F", 8))
            act(sinkE[:, :], Fr[:, 8, 0:12], AF.Exp, [("F", 8)], ["sinkE"])

        ncnt = [0]

        def tok_blocks(segs):
            out = []
            for (off, ln) in segs:
                for t in range(ln // 128):
                    out.append(off + t * 128)
            return out

        def mixer_in_tile(l, segs, v, full, prompt):
            norm_tile(segs, v, 1)
            wv = wview(win_d[l])
            xr = [("xT", c) for c in range(NCH)]
            tbs = tok_blocks(segs)
            RC, RSn, XC0, TV = 0, 1, 2, RS
            if not prompt:
                lo = 0
                for (off, ln) in segs:
                    dma("sp", Fr[:, RC, lo:lo + ln], ropeC_d[:, off:off + ln], (), [("F", RC)], ("F", RC))
                    dma("sp", Fr[:, RSn, lo:lo + ln], ropeS_d[:, off:off + ln], (), [("F", RSn)], ("F", RSn))
                    dma("sp", Fr[:, TV, lo:lo + ln], tokv_d[:, off:off + ln], (), [("F", TV)], ("F", TV))
                    lo += ln

            def proj_fm(s, j, bk):
                for c in range(NCH):
                    mm(ps[bk][:, :], WS[s][:, c, j * 128:(j + 1) * 128], xT[:, c, :], c == 0, c == NCH - 1,
                       [("W", s), ("xT", c)], [("ps", bk)])

            def qk_slot(col0, kind, head0):
                s = wload(wv, col0, 0, 32, ("win", l, col0))
                for j in range(2):
                    n = ncnt[0]
                    ncnt[0] += 1
                    bk = n % 4
                    proj_fm(s, j, bk)
                    head = head0 + j
                    isq = kind in ("qa", "qb")
                    rope = (kind in ("qa", "ka")) and not prompt
                    gcol = R_QKN + l * 4 + {"qa": 0, "ka": 1, "qb": 2, "kb": 3}[kind]
                    gv = smallT[:, gcol:gcol + 1]
                    qf, t, nb, sq = 11 + n % 2, TT_[n % 2], 4 + n % 2, n % 2
                    act(Fs(qf), ps[bk][:, :], AF.Identity, [("ps", bk)], [("F", qf)])
                    act(SQ[:, sq, :], ps[bk][:, :], AF.Square, [("ps", bk)], [("SQ", sq)])
                    mm(ps[nb][:, :], ones[:], SQ[:, sq, :], True, True, ["ones", ("SQ", sq)], [("ps", nb)])
                    act(Fs(t), ps[nb][:, :], AF.Sqrt, [("ps", nb)], [("F", t)], bias=EPS, scale=1.0 / HD)
                    rcp(Fs(t), Fs(t), [("F", t)], [("F", t)])
                    stg = n % 2
                    need_f32 = rope or (prompt and not isq)
                    if not need_f32:
                        stt(STG[stg], Fs(qf), gv, Fs(t), ALU.mult, ALU.mult, [("F", qf), ("F", t), "smallT"], [("STG", stg)])
                    else:
                        stt(Fs(qf), Fs(qf), gv, Fs(t), ALU.mult, ALU.mult, [("F", qf), ("F", t), "smallT"], [("F", qf)])
                        if rope:
                            act(SQ[:, sq, :], Fs(qf), AF.Identity, [("F", qf)], [("SQ", sq)])
                            mm(ps[6][:, :], swapm[:], SQ[:, sq, :], True, True, ["swapm", ("SQ", sq)], [("ps", 6)])
                            tt(Fs(13), ps[6][:, :], Fs(RSn), ALU.mult, [("ps", 6), ("F", RSn)], [("F", 13)])
                            tt(Fs(qf), Fs(qf), Fs(RC), ALU.mult, [("F", qf), ("F", RC)], [("F", qf)])
                            tt(STG[stg], Fs(qf), Fs(13), ALU.add, [("F", qf), ("F", 13)], [("STG", stg)])
                        else:
                            act(STG[stg], Fs(qf), AF.Identity, [("F", qf)], [("STG", stg)])
                    dst, rn, hidx = (qT_s, "qT", head) if isq else (kT_s, "kT", head)
                    lo = 0
                    for (off, ln) in segs:
                        dma("sp", dst[hidx, :, off:off + ln], STG[stg][:, lo:lo + ln], [("STG", stg)],
                            sres(rn, hidx, [(off, ln)]), ("STG", stg))
                        lo += ln
                    if prompt and not isq:
                        for tb in range(4):
                            tr(ps[7][:, tb * 128:(tb + 1) * 128], Fr[:, qf, tb * 128:(tb + 1) * 128], [("F", qf)], [("ps", 7)])
                        hs = HS[n % 4]
                        cp(Fs(hs), ps[7][:, :], [("ps", 7)], [("F", hs)])
                        od, hc = (nak_d, head) if kind == "ka" else (nbk_d, head - 4)
                        for tb in range(4):
                            dma("sp", od[tb // 2, l, (tb % 2) * 128:(tb % 2 + 1) * 128, hc * 128:(hc + 1) * 128],
                                Fr[:, hs, tb * 128:(tb + 1) * 128], [("F", hs)], [("ok", kind, head, tb)], ("Fst", hs))

            def v_slot(col0, head0):
                s = wload(wv, col0, 0, 32, ("win", l, col0))
                for j in range(2):
                    n = ncnt[0]
                    ncnt[0] += 1
                    bk = n % 4
                    proj_fm(s, j, bk)
                    head = head0 + j
                    qf = 11 + n % 2
                    act(Fs(qf), ps[bk][:, :], AF.Identity, [("ps", bk)], [("F", qf)])
                    for tb in range(4):
                        tr(ps[7][:, tb * 128:(tb + 1) * 128], Fr[:, qf, tb * 128:(tb + 1) * 128], [("F", qf)], [("ps", 7)])
                    stg = n % 2
                    cp(STG[stg], ps[7][:, :], [("ps", 7)], [("STG", stg)])
                    lo = 0
                    for (off, ln) in segs:
                        g0, cnt = off // 128, ln // 128
                        dma("sp", V_s[head, :, g0:g0 + cnt, :], STG[stg][:, lo:lo + ln].rearrange("p (k d) -> p k d", d=128),
                            [("STG", stg)], sres("V", head, [(off, ln)]), ("STG", stg))
                        lo += ln
                    if prompt:
                        hs = HS[n % 4]
                        cp(Fs(hs), ps[7][:, :], [("ps", 7)], [("F", hs)])
                        od, hc = (nav_d, head) if head < 4 else (nbv_d, head - 4)
                        for tb in range(4):
                            dma("sp", od[tb // 2, l, (tb % 2) * 128:(tb % 2 + 1) * 128, hc * 128:(hc + 1) * 128],
                                Fr[:, hs, tb * 128:(tb + 1) * 128], [("F", hs)], [("ov", head, tb)], ("Fst", hs))

            def conv_slots(i):
                s = wload(wv, 7168 + i * 256, 0, 32, ("win", l, 7168 + i * 256))
                for j in range(2):
                    n = ncnt[0]
                    ncnt[0] += 1
                    bk = n % 4
                    proj_fm(s, j, bk)
                    act(Fs(XC0 + j), ps[bk][:, :], AF.Identity, [("ps", bk)], [("F", XC0 + j)])
                if full:
                    s = wload(wv, 8192 + i * 256, 0, 32, ("win", l, 8192 + i * 256))
                    for j in range(2):
                        n = ncnt[0]
                        ncnt[0] += 1
                        bk = n % 4
                        proj_fm(s, j, bk)
                        hs = HS[n % 4]
                        cp(Fs(hs), ps[bk][:, :], [("ps", bk)], [("F", hs)])
                        seg_dma_out(hs, gbT_s, 2 * i + j, segs, "gb")
                s = wload(wv, 9216 + i * 256, 0, 32, ("win", l, 9216 + i * 256))
                for j in range(2):
                    n = ncnt[0]
                    ncnt[0] += 1
                    bk = n % 4
                    proj_fm(s, j, bk)
                    hs = HS[n % 4]
                    tt(Fs(hs), ps[bk][:, :], Fs(XC0 + j), ALU.mult, [("ps", bk), ("F", XC0 + j)], [("F", hs)])
                    if not prompt:
                        tt(Fs(hs), Fs(hs), Fs(TV), ALU.mult, [("F", hs), ("F", TV)], [("F", hs)])
                    seg_dma_out(hs, uT_s, 2 * i + j, segs, "u")

            if full and MSUB & 1:
                for i in range(6):
                    qk_slot(i * 256, "qa", 2 * i)
            for i in range(2 if MSUB & 2 else 0):
                qk_slot(1536 + i * 256, "ka", 2 * i)
            for i in range(2 if MSUB & 4 else 0):
                v_slot(2048 + i * 256, 2 * i)
            if full and MSUB & 8:
                for i in range(6):
                    qk_slot(2560 + i * 256, "qb", 12 + 2 * i)
            for i in range(6 if MSUB & 16 else 0):
                qk_slot(4096 + i * 256, "kb", 4 + 2 * i)
            for i in range(6 if MSUB & 32 else 0):
                v_slot(5632 + i * 256, 4 + 2 * i)
            for i in range(4 if MSUB & 64 else 0):
                conv_slots(i)

        acnt = [0]

        def attn_tile(l, qoff, v, prompt):
            b0 = qoff // 128
            segs = [(qoff, 512)]
            if not prompt:
                dma("sp", mAt, maskA_d[b0:b0 + 4, :, :].rearrange("q p f -> p q f"), (), ["mAt"], "mAt")
                dma("sp", mBt, maskB_d[b0:b0 + 4, :, :].rearrange("q p f -> p q f"), (), ["mBt"], "mBt")
            for hd in range(24):
                isA = hd < 12
                h = hd if isA else hd - 12
                kvh = h // 3 if isA else 4 + h
                DL = 1 if isA else 3
                bf = hd % 2
                dma("sp", qTh[bf], qT_s[hd, :, qoff:qoff + 512], sres("qT", hd, segs), [("qTh", bf)], ("qTh", bf))
                if prompt:
                    kb_lo, kb_hi = 16, 19
                else:
                    kb_lo, kb_hi = max(b0 - DL, 0), min(b0 + 3 + DL, 15)
                nkb = kb_hi - kb_lo + 1
                kseg = [(kb_lo * 128, nkb * 128)]
                dma("sp", KTw[bf][:, 0:nkb * 128], kT_s[kvh, :, kb_lo * 128:(kb_hi + 1) * 128], sres("kT", kvh, kseg),
                    [("KTw", bf)], ("KTw", bf))
                dma("sp", Vw[bf][:, 0:nkb, :], V_s[kvh, :, kb_lo:kb_hi + 1, :], sres("V", kvh, kseg), [("Vw", bf)], ("Vw", bf))
                if not prompt:
                    dma("sp", cK[bf], ckT_s[kvh, :, :], [("ckT", kvh)], [("cK", bf)], ("cK", bf))
                    cvs = 4 + bf
                    csrc = cav_d[l][:, kvh * 128:(kvh + 1) * 128] if isA else cbv_d[l][:, (kvh - 4) * 128:(kvh - 3) * 128]
                    dma("sp", Fs(cvs).rearrange("p (k d) -> p k d", d=128), csrc.rearrange("(k p) d -> p k d", p=128), (),
                        [("F", cvs)], ("F", cvs))
                    act(cVb[bf].rearrange("p k d -> p (k d)"), Fs(cvs), AF.Identity, [("F", cvs)], [("cVb", bf)])
                for qi in range(4):
                    n = acnt[0]
                    acnt[0] += 1
                    pb = n % 2
                    b = b0 + qi
                    qcols = qTh[bf][:, qi * 128:(qi + 1) * 128]
                    if prompt:
                        sq_ = qi // 2
                        local = [(16 + 2 * sq_, None), (17 + 2 * sq_, None)]
                    else:
                        local = [(b + d, d) for d in range(-DL, DL + 1) if 0 <= b + d <= 15]
                    nl = len(local)
                    es_ = F2(0) if pb == 0 else F2(2)
                    esr = [("F", 0), ("F", 1)] if pb == 0 else [("F", 2), ("F", 3)]
                    for idx, (kb, d) in enumerate(local):
                        bank = pb * 3 + idx // 4
                        col = (idx % 4) * 128
                        mm(ps[bank][:, col:col + 128], KTw[bf][:, (kb - kb_lo) * 128:(kb - kb_lo + 1) * 128], qcols, True, True,
                           [("KTw", bf), ("qTh", bf)], [("ps", bank)])
                    groups = [(0, min(nl, 4))] + ([(4, nl)] if nl > 4 else [])
                    for gi, (i0, i1) in enumerate(groups):
                        bank = pb * 3 + gi
                        wd = (i1 - i0) * 128
                        if prompt:
                            act(PT[pb][:, i0 * 128:i1 * 128], ps[bank][:, 0:wd], AF.Exp, [("ps", bank)], [("PT", pb)], scale=SCALE)
                        else:
                            act(es_[:, i0 * 128:i1 * 128], ps[bank][:, 0:wd], AF.Exp, [("ps", bank)], esr, scale=SCALE)
                            mc = (local[i0][1] + DL) * 128
                            if isA:
                                tt(PT[pb][:, i0 * 128:i1 * 128], es_[:, i0 * 128:i1 * 128], mAt[:, qi, mc:mc + wd], ALU.mult,
                                   esr + ["mAt"], [("PT", pb)])
                            else:
                                tt(es_[:, i0 * 128:i1 * 128], es_[:, i0 * 128:i1 * 128], EBv[:, h, mc:mc + wd], ALU.mult,
                                   esr + ["EB"], esr)
                                tt(PT[pb][:, i0 * 128:i1 * 128], es_[:, i0 * 128:i1 * 128], mBt[:, qi, mc:mc + wd], ALU.mult,
                                   esr + ["mBt"], [("PT", pb)])
                    tiles = [(Vw[bf][:, kb - kb_lo, :], PT[pb][:, i * 128:(i + 1) * 128]) for i, (kb, d) in enumerate(local)]
                    rr = [("Vw", bf), ("PT", pb), "ones"]
                    if not prompt:
                        bank = pb * 3 + 2
                        for j in range(4):
                            mm(ps[bank][:, j * 128:(j + 1) * 128], cK[bf][:, j * 128:(j + 1) * 128], qcols, True, True,
                               [("cK", bf), ("qTh", bf)], [("ps", bank)])
                        act(PT[pb][:, 896:1408], ps[bank][:, :], AF.Exp, [("ps", bank)], [("PT", pb)], scale=SCALE)
                        tiles += [(cVb[bf][:, j, :], PT[pb][:, 896 + j * 128:896 + (j + 1) * 128]) for j in range(4)]
                        rr.append(("cVb", bf))
                    oc = (n % 4) * 128
                    for i, (vap, pap) in enumerate(tiles):
                        mm(ps[6][:, oc:oc + 128], vap, pap, i == 0, i == len(tiles) - 1, rr, [("ps", 6)])
                        mm(ps[7][:, oc:oc + 128], ones[:], pap, i == 0, i == len(tiles) - 1, rr, [("ps", 7)])
                    t = TT_[n % 2]
                    if isA:
                        ts2(Fr[:, t, 0:128], ps[7][:, oc:oc + 128], sinkE[:, h:h + 1], 1.0, ALU.add, ALU.mult,
                            [("ps", 7), "sinkE"], [("F", t)])
                        rcp(Fr[:, t, 0:128], Fr[:, t, 0:128], [("F", t)], [("F", t)])
                    else:
                        rcp(Fr[:, t, 0:128], ps[7][:, oc:oc + 128], [("ps", 7)], [("F", t)])
                    tt(xT[:, hd, qi * 128:(qi + 1) * 128], ps[6][:, oc:oc + 128], Fr[:, t, 0:128], ALU.mult,
                       [("ps", 6), ("F", t)], [("xT", hd)])
            U = F2(10)
            for ch in range(8):
                w0, w1, w2 = [smallT[:, R_CONVW + l * 24 + j * 8 + ch:R_CONVW + l * 24 + j * 8 + ch + 1] for j in range(3)]
                cb = smallT[:, R_CONVB + l * 8 + ch:R_CONVB + l * 8 + ch + 1]
                ur = [("F", 10), ("F", 11)]
                dma("sp", Fs(12), gbT_s[ch, :, qoff:qoff + 512], sres("gb", ch, segs), [("F", 12)], ("F", 12))
                if not prompt:
                    dma("sp", U[:, 0:514], uT_s[ch, :, qoff - 1:qoff + 513], sres("u", ch, [(qoff - 1, 514)]), ur, ("F", 10))
                    parts = [(0, 512)]
                else:
                    parts = [(0, 256), (256, 256)]
                for (p0, pl) in parts:
                    if prompt:
                        mset(U[:, 0:1], 0.0, ur)
                        mset(U[:, pl + 1:pl + 2], 0.0, ur)
                        dma("sp", U[:, 1:pl + 1], uT_s[ch, :, qoff + p0:qoff + p0 + pl], sres("u", ch, [(qoff + p0, pl)]), ur,
                            ("F", 10))
                    y = Fr[:, 13, p0:p0 + pl]
                    ts2(y, U[:, 1:pl + 1], w1, cb, ALU.mult, ALU.add, ur + ["smallT"], [("F", 13)])
                    stt(y, U[:, 0:pl], w0, y, ALU.mult, ALU.add, ur + ["smallT", ("F", 13)], [("F", 13)])
                    stt(y, U[:, 2:pl + 2], w2, y, ALU.mult, ALU.add, ur + ["smallT", ("F", 13)], [("F", 13)])
                    tt(xT[:, 24 + ch, p0:p0 + pl], y, Fr[:, 12, p0:p0 + pl], ALU.mult, [("F", 13), ("F", 12)], [("xT", 24 + ch)])
            wvo = wview(wout_d[l])
            for mi in range(16):
                s = wload(wvo, mi * 256, 0, 32, ("wout", l, mi))
                for mb in range(2):
                    m = 2 * mi + mb
                    bk = m % 4
                    for c in range(NCH):
                        mm(ps[bk][:, :], WS[s][:, c, mb * 128:(mb + 1) * 128], xT[:, c, :], c == 0, c == NCH - 1,
                           [("W", s), ("xT", c)], [("ps", bk)])
                    sl = 6 + m % 2
                    load_h(m, sl, segs)
                    stt(Fs(sl), ps[bk][:, :], modG(v, 1, m), Fs(sl), ALU.mult, ALU.add, [("ps", bk), "DER", ("F", sl)], [("F", sl)])
                    store_h(m, sl, segs)

        PSEG = [(POFF, 512)]

        def sseg(off):
            return [(off, 512)]

        MODT, DER = MODTs[0], DERs[0]
        if stage != 99:
            mset(MODT[:, :, :], 0.0, ["MODT"])
            for v in range(2):
                for s3 in range(3):
                    mset(MODT[:, v, (3 * s3 + 2) * 32:(3 * s3 + 3) * 32], 1.0, ["MODT"])
                    g0 = R_NORM + s3 * 32
                    ts2(DER[:, v, s3, 0, :], MODT[:, v, (3 * s3 + 1) * 32:(3 * s3 + 2) * 32], 1.0, 1.0, ALU.add, ALU.mult,
                        ["MODT"], ["DER"])
                    tt(DER[:, v, s3, 0, :], DER[:, v, s3, 0, :], smallT[:, g0:g0 + 32], ALU.mult, ["DER", "smallT"], ["DER"])
                    ts2(DER[:, v, s3, 1, :], MODT[:, v, (3 * s3 + 2) * 32:(3 * s3 + 3) * 32], 1.0, 0.0,
                        ALU.mult, ALU.add, ["MODT"], ["DER"])
            arena_fence()
            if stage not in (20, 21, 30):
                mixer_prep(0)
            if stage not in (20, 22, 30):
                mixer_in_tile(0, PSEG, 0, True, True)
            if stage >= 11 and stage < 20:
                mixer_in_tile(0, sseg(256), 1, True, False)
                mixer_in_tile(0, sseg(768), 1, True, False)
                mixer_in_tile(0, [(0, 256), (1792, 256)], 1, False, False)
            if stage not in (21, 22, 30):
                attn_tile(0, POFF, 0, True)
            if stage >= 11 and stage < 20:
                attn_tile(0, 256, 1, False)
        modcnt = [0]

        def mod1_hook(mi):
            for _ in range(3):
                if modcnt[0] < 144:
                    mod_slot(1, modcnt[0])
                    modcnt[0] += 1

        for l in (range(NL) if stage == 99 else ()):
            curl[0] = l
            if l == 0:
                modulation(0)
            else:
                while modcnt[0] < 144:
                    mod_slot(1, modcnt[0])
                    modcnt[0] += 1
                mod_finish(1)
            w1 = [0, 512, 1024, 1536] if l == 0 else [256, 768, 1280]
            wq = [256, 768, 1280] if l == 0 else [512, 1024]
            kvo = [(0, 256), (1792, 256)] if l == 0 else [(256, 256), (1536, 256)]
            ffn_tile(l, 0, PSEG, 0)
            for w in w1:
                ffn_tile(l, 0, sseg(w), 1)
            arena_fence()
            mixer_prep(l)
            mixer_in_tile(l, PSEG, 0, True, True)
            for w in wq:
                mixer_in_tile(l, sseg(w), 1, True, False)
            mixer_in_tile(l, kvo, 1, False, False)
            attn_tile(l, POFF, 0, True)
            for w in wq:
                attn_tile(l, w, 1, False)
            arena_fence()
            ffn_tile(l, 1, PSEG, 0)
            for w in wq:
                ffn_tile(l, 1, sseg(w), 1, mod1_hook if l == 0 else None)

        for (off, dst, r0) in ((512, ys_d, 0), (1024, ys_d, 512), (POFF, yp_d, 0)):
            for cg in range(8):
                for cc in range(4):
                    load_h(cg * 4 + cc, HL[cc], [(off, 512)])
                for tb in range(4):
                    bk = 4 + tb % 4
                    for cc in range(4):
                        tr(ps[bk][:, cc * 128:(cc + 1) * 128], Fr[:, HL[cc], tb * 128:(tb + 1) * 128],
                           [("F", HL[cc])], [("ps", bk)])
                    hs = HS[tb]
                    cp(Fs(hs), ps[bk][:, :], [("ps", bk)], [("F", hs)])
                    dma("sp", dst[r0 + tb * 128:r0 + (tb + 1) * 128, cg * 512:(cg + 1) * 512], Fs(hs),
                        [("F", hs)], [("y", off, tb, cg)], ("Fst", hs))

        S.emit(nc, block, es)
    return nc


def _small_table(c_ctx, c_mine, mod_b, norm_ffn1, norm_mix, norm_ffn2, conv_w, conv_b, a_q_norm, a_k_norm, b_q_norm, b_k_norm):
    sm = np.zeros((R_TOT, 128), np.float32)
    sm[R_CVEC:R_CVEC + 32] = c_ctx.reshape(32, 128)
    sm[R_CVEC + 32:R_CVEC + 64] = c_mine.reshape(32, 128)
    for l in range(NL):
        sm[R_MODB + l * 288:R_MODB + (l + 1) * 288] = mod_b[l].reshape(288, 128)
        for s3, nm in enumerate((norm_ffn1, norm_mix, norm_ffn2)):
            sm[R_NORM + (l * 3 + s3) * 32:R_NORM + (l * 3 + s3 + 1) * 32] = nm[l].reshape(32, 128)
        sm[R_CONVW + l * 24:R_CONVW + (l + 1) * 24] = conv_w[l].reshape(24, 128)
        sm[R_CONVB + l * 8:R_CONVB + (l + 1) * 8] = conv_b[l].reshape(8, 128)
        for j, q in enumerate((a_q_norm, a_k_norm, b_q_norm, b_k_norm)):
            sm[R_QKN + l * 4 + j] = q[l]
    return sm


def _pos_consts(qd):
    base_row = 16 * qd - 8
    t = np.arange(2048)
    tg = base_row * 64 + t
    row = np.floor_divide(tg, 64)
    col = np.mod(tg, 64)
    valid_t = (row >= 0) & (row < 64)
    inv = (10000.0 ** (-np.arange(32, dtype=np.float32) * 2.0 / 64.0)).astype(np.float32)
    d = np.arange(128)
    axis, within = d // 64, d % 64
    pair = within % 32
    sign = np.where(within < 32, -1.0, 1.0).astype(np.float32)
    pos = np.where(axis[:, None] == 0, row[None, :], col[None, :]).astype(np.float32)
    ang = (pos * inv[pair][:, None]).astype(np.float32)
    ropeC = np.cos(ang).astype(np.float32)
    ropeS = (np.sin(ang).astype(np.float32) * sign[:, None]).astype(np.float32)
    m = np.arange(128)
    partner = np.where((m % 64) < 32, m + 32, m - 32)
    swapm = np.zeros((128, 128), np.float32)
    swapm[partner, m] = 1.0
    k = np.arange(128)[:, None]
    q = np.arange(128)[None, :]
    maskA = np.zeros((16, 128, 3, 128), np.float32)
    maskB = np.zeros((16, 128, 7, 128), np.float32)
    for bq in range(16):
        qt = bq * 128 + q
        qrow = base_row + qt // 64
        qc = qt % 64
        rs = np.clip(qrow - 4, 0, 56)
        cs = np.clip(qc - 8, 0, 48)
        for dl in range(-3, 4):
            kb = bq + dl
            if kb < 0 or kb > 15:
                continue
            kt = kb * 128 + k
            krow = base_row + kt // 64
            kc = kt % 64
            kval = (krow >= 0) & (krow < 64)
            if -1 <= dl <= 1:
                maskA[bq, :, dl + 1, :] = (np.abs(qt - kt) <= 128) & kval
            maskB[bq, :, dl + 3, :] = kval & (krow >= rs) & (krow < rs + 8) & (kc >= cs) & (kc < cs + 16)
    bf = ml_dtypes.bfloat16
    return {"ropeC": ropeC, "ropeS": ropeS, "swapm": swapm.astype(bf),
            "tokvalid": np.ascontiguousarray(np.broadcast_to(valid_t.astype(np.float32)[None, :], (128, 2048))),
            "maskA": maskA.reshape(16, 128, 384).astype(bf), "maskB": maskB.reshape(16, 128, 896).astype(bf)}


def _core_inputs(i, inp, pc, shared):
    b, qd = i // 4, i % 4
    r0 = 16 * qd
    xs = np.zeros((2048, D), np.float32)
    lo, hi = max(r0 - 8, 0), min(r0 + 24, 64)
    xs[(lo - (r0 - 8)) * 64:(hi - (r0 - 8)) * 64] = inp["x_sample"][b, lo * 64:hi * 64]
    xp = np.ascontiguousarray(inp["x_prompt"][2 * i:2 * i + 2].reshape(512, D))
    sm = _small_table(inp["c_ctx"], inp["c"][b], inp["mod_b"], inp["norm_ffn1"], inp["norm_mix"], inp["norm_ffn2"],
                      inp["conv_w"], inp["conv_b"], inp["a_q_norm"], inp["a_k_norm"], inp["b_q_norm"], inp["b_k_norm"])
    m = {"xs": xs, "xp": xp, "small": sm, "ident": np.eye(128, dtype=np.float32),
         "cak": np.ascontiguousarray(inp["cache_a_k"][b].reshape(NL, 512, 512)),
         "cav": np.ascontiguousarray(inp["cache_a_v"][b].reshape(NL, 512, 512)),
         "cbk": np.ascontiguousarray(inp["cache_b_k"][b].reshape(NL, 512, 1536)),
         "cbv": np.ascontiguousarray(inp["cache_b_v"][b].reshape(NL, 512, 1536))}
    m.update(pc[qd])
    m.update(shared)
    return m


def run(inp, stage=99, trace=False):
    inp = {k: np.asarray(v) for k, v in inp.items()}
    nc = build(stage)
    shared = {k: np.ascontiguousarray(inp[k]) for k in ("mod_w", "ffn1_w13", "ffn1_w2", "ffn2_w13", "ffn2_w2", "w_in", "w_out")
              if k in DECL}
    kc = np.arange(64)[:, None]
    qc = np.arange(64)[None, :]
    dc = np.clip(kc - qc, -15, 15) + 15
    rp = inp["b_rpb"]
    shared["Tz"] = np.ascontiguousarray(np.transpose(rp[:, :, :, dc], (0, 2, 3, 1, 4)))
    shared["sinkb"] = np.ascontiguousarray(np.broadcast_to(inp["a_sink"][:, None, :], (NL, 128, 12)))
    pc = [_pos_consts(qd) for qd in range(4)]
    in_maps = [_core_inputs(i, inp, pc, shared) for i in range(8)]
    in_maps = [{k: v for k, v in m.items() if k in DECL} for m in in_maps]
    return run_bass_kernel_spmd(nc, in_maps, core_ids=list(range(8)), trace=trace)


def kernel(**inputs):
    r = run(inputs).results
    yp = np.concatenate([r[i]["yp"].reshape(2, 256, D) for i in range(8)], axis=0)
    ys = np.stack([np.concatenate([r[4 * b + q]["ys"] for q in range(4)], axis=0) for b in range(2)], axis=0)
    nak = np.concatenate([r[i]["nak"].reshape(2, NL, 256, 4, 128) for i in range(8)], axis=0)
    nav = np.concatenate([r[i]["nav"].reshape(2, NL, 256, 4, 128) for i in range(8)], axis=0)
    nbk = np.concatenate([r[i]["nbk"].reshape(2, NL, 256, 12, 128) for i in range(8)], axis=0)
    nbv = np.concatenate([r[i]["nbv"].reshape(2, NL, 256, 12, 128) for i in range(8)], axis=0)
    return yp, ys, nak, nav, nbk, nbv
```
